# Optimizing a Trainium2 kernel written in Bass

```python
import math
import jax, jax.numpy as jnp
from jax import lax
import numpy as np

D_MODEL = 1024
BATCH = 4
SEQ = 8192
DEPTH = 1

PLE_DIM = 256
MIX_WIDTH = D_MODEL
ATTN_WIDTH = MIX_WIDTH // 2
SGU_WIDTH = MIX_WIDTH - ATTN_WIDTH
HEAD_DIM = 64
N_ATTN_HEADS = ATTN_WIDTH // HEAD_DIM
N_SGU_GROUPS = 4
SGU_GROUP_DIM = SGU_WIDTH // N_SGU_GROUPS
SGU_CHUNK = 128
DILATION_PAIRS = ((128, 1), (512, 4), (2048, 16))
QBLK = 128
D_FF = ((8 * D_MODEL + 3 * 256 - 1) // (3 * 256)) * 256
PROJ_COLS = 3 * ATTN_WIDTH + 2 * SGU_WIDTH
EPS = 1e-6
NEG = -1e30

kernel_name = "hybrid_dilated_attn_gmlp_block"


def rmsnorm(x, g):
    xf = x.astype(jnp.float32)
    y = xf * lax.rsqrt(jnp.mean(xf * xf, axis=-1, keepdims=True) + EPS)
    return (y * g.astype(jnp.float32)).astype(x.dtype)


def layernorm(x, g, b):
    xf = x.astype(jnp.float32)
    mu = jnp.mean(xf, axis=-1, keepdims=True)
    xc = xf - mu
    y = xc * lax.rsqrt(jnp.mean(xc * xc, axis=-1, keepdims=True) + EPS)
    return (y * g.astype(jnp.float32) + b.astype(jnp.float32)).astype(x.dtype)


def dilated_branch(q, k, v, slopes, window, dilation):
    B, H, S, hd = q.shape
    span = dilation * QBLK
    s_pad = -(-S // span) * span
    M = s_pad // dilation
    nb = M // QBLK
    n_steps = window // dilation
    pad = ((0, 0), (0, 0), (0, s_pad - S), (0, 0))

    def to_blocks(t):
        t = jnp.pad(t, pad).reshape(B, H, M, dilation, hd).transpose(0, 1, 3, 2, 4)
        return t.reshape(B, H, dilation, nb, QBLK, hd)

    def with_prev(t):
        prev = jnp.pad(t, ((0, 0), (0, 0), (0, 0), (1, 0), (0, 0), (0, 0)))[:, :, :, :-1]
        return jnp.concatenate([prev, t], axis=-2)

    qb = to_blocks(q)
    kc = with_prev(to_blocks(k))
    vc = with_prev(to_blocks(v))
    s = jnp.einsum('bhrnqc,bhrnkc->bhrnqk', qb, kc)

    qi = jnp.arange(QBLK)[:, None]
    ki = jnp.arange(2 * QBLK)[None, :]
    steps = QBLK + qi - ki
    blk = jnp.arange(nb)[:, None, None]
    valid = (steps >= 0) & (steps <= n_steps) & (blk * QBLK - QBLK + ki >= 0)
    dist = (jnp.clip(steps, 0, None) * dilation).astype(jnp.float32)
    bias = -slopes[:, None, None] * dist[None]
    s = s + bias[None, :, None, None]
    s = jnp.where(valid[None, None, None], s, NEG)
    mx = jnp.max(s, axis=-1, keepdims=True)
    e = jnp.exp(s - mx)
    den = jnp.sum(e, axis=-1)
    o = jnp.einsum('bhrnqk,bhrnkc->bhrnqc', e, vc) / den[..., None]
    lse = mx[..., 0] + jnp.log(den)
    o = o.reshape(B, H, dilation, M, hd).transpose(0, 1, 3, 2, 4).reshape(B, H, s_pad, hd)[:, :, :S]
    lse = lse.reshape(B, H, dilation, M).transpose(0, 1, 3, 2).reshape(B, H, s_pad)[:, :, :S]
    return o, lse


def dilated_attention(q, k, v):
    B, S, _ = q.shape
    dtype = q.dtype

    def heads(t):
        return t.reshape(B, S, N_ATTN_HEADS, HEAD_DIM).transpose(0, 2, 1, 3).astype(jnp.float32)

    qh = heads(q) * (HEAD_DIM ** -0.5)
    kh, vh = heads(k), heads(v)
    slopes = 2.0 ** (-8.0 * (jnp.arange(N_ATTN_HEADS, dtype=jnp.float32) + 1.0) / N_ATTN_HEADS)
    outs, lses = [], []
    for window, dilation in DILATION_PAIRS:
        o, l = dilated_branch(qh, kh, vh, slopes, window, dilation)
        outs.append(o)
        lses.append(l)
    w = jax.nn.softmax(jnp.stack(lses, axis=0), axis=0)
    out = jnp.sum(w[..., None] * jnp.stack(outs, axis=0), axis=0)
    return out.transpose(0, 2, 1, 3).reshape(B, S, ATTN_WIDTH).astype(dtype)


def spatial_gating(u, z, ln_g, ln_b, w_s, b_s):
    B, S, _ = u.shape
    nc = S // SGU_CHUNK
    u = jax.nn.gelu(u).reshape(B, S, N_SGU_GROUPS, SGU_GROUP_DIM)
    z = jax.nn.gelu(z).reshape(B, S, N_SGU_GROUPS, SGU_GROUP_DIM)
    z = layernorm(z, ln_g, ln_b)
    zc = z.reshape(B, nc, SGU_CHUNK, N_SGU_GROUPS, SGU_GROUP_DIM)
    causal = jnp.tril(jnp.ones((SGU_CHUNK, SGU_CHUNK), dtype=w_s.dtype))
    wm = w_s * causal[None]
    mixed = jnp.einsum('gij,bnjgc->bnigc', wm, zc) + b_s.T[None, None, :, :, None]
    out = u * mixed.reshape(B, S, N_SGU_GROUPS, SGU_GROUP_DIM)
    return out.reshape(B, S, SGU_WIDTH)


def setup_inputs(seed: int = 0) -> dict:
    key = jax.random.key(seed)
    ks = jax.random.split(key, 20)
    f32 = jnp.float32

    def nrm(k, shape, scale):
        return jax.random.normal(k, shape, f32) * scale

    def gain(k, shape):
        return 1.0 + 0.05 * jax.random.normal(k, shape, f32)

    L = DEPTH
    return {
        "x": jax.random.normal(ks[0], (BATCH, SEQ, D_MODEL), f32),
        "p": jax.random.normal(ks[1], (DEPTH, BATCH, SEQ, PLE_DIM), f32),
        "ln_pre_mix": gain(ks[2], (L, D_MODEL)),
        "w_in": nrm(ks[3], (L, D_MODEL, PROJ_COLS), D_MODEL ** -0.5),
        "sgu_ln_g": gain(ks[4], (L, SGU_GROUP_DIM)),
        "sgu_ln_b": nrm(ks[5], (L, SGU_GROUP_DIM), 0.02),
        "w_spatial": nrm(ks[6], (L, N_SGU_GROUPS, SGU_CHUNK, SGU_CHUNK), SGU_CHUNK ** -0.5),
        "b_spatial": gain(ks[7], (L, N_SGU_GROUPS, SGU_CHUNK)),
        "attn_out_norm": gain(ks[8], (L, ATTN_WIDTH)),
        "sgu_out_norm": gain(ks[9], (L, SGU_WIDTH)),
        "w_out": nrm(ks[10], (L, MIX_WIDTH, D_MODEL), MIX_WIDTH ** -0.5),
        "ln_post_mix": gain(ks[11], (L, D_MODEL)),
        "ln_pre_ffn": gain(ks[12], (L, D_MODEL)),
        "w_gate_up": nrm(ks[13], (L, D_MODEL, 2 * D_FF), D_MODEL ** -0.5),
        "w_down": nrm(ks[14], (L, D_FF, D_MODEL), D_FF ** -0.5),
        "ln_post_ffn": gain(ks[15], (L, D_MODEL)),
        "w_pe_gate": nrm(ks[16], (L, D_MODEL, D_MODEL), D_MODEL ** -0.5),
        "b_pe_gate": nrm(ks[17], (L, D_MODEL), 0.02),
        "w_pe_proj": nrm(ks[18], (L, PLE_DIM, D_MODEL), PLE_DIM ** -0.5),
    }


def reference(x, p, ln_pre_mix, w_in, sgu_ln_g, sgu_ln_b, w_spatial, b_spatial,
              attn_out_norm, sgu_out_norm, w_out, ln_post_mix, ln_pre_ffn, w_gate_up,
              w_down, ln_post_ffn, w_pe_gate, b_pe_gate, w_pe_proj):
    h = x
    splits = [ATTN_WIDTH, 2 * ATTN_WIDTH, 3 * ATTN_WIDTH, 3 * ATTN_WIDTH + SGU_WIDTH]
    for i in range(DEPTH):
        a = rmsnorm(h, ln_pre_mix[i])
        proj = a @ w_in[i]
        q, k, v, u, z = jnp.split(proj, splits, axis=-1)
        attn = dilated_attention(q, k, v)
        sgu = spatial_gating(u, z, sgu_ln_g[i], sgu_ln_b[i], w_spatial[i], b_spatial[i])
        groups = jnp.concatenate([rmsnorm(attn, attn_out_norm[i]),
                                  rmsnorm(sgu, sgu_out_norm[i])], axis=-1)
        mixed = groups @ w_out[i]
        h = h + rmsnorm(mixed, ln_post_mix[i])
        f = rmsnorm(h, ln_pre_ffn[i])
        g, up = jnp.split(f @ w_gate_up[i], 2, axis=-1)
        y = (jax.nn.silu(g) * up) @ w_down[i]
        h = h + rmsnorm(y, ln_post_ffn[i])
        gate = jax.nn.sigmoid(h @ w_pe_gate[i] + b_pe_gate[i])
        h = h + gate * (p[i] @ w_pe_proj[i])
    return h
```

```python
import numpy as np
from contextlib import ExitStack
import concourse.bass as bass
import concourse.mybir as mybir
from concourse.bass_utils import run_bass_kernel_spmd

F32 = mybir.dt.float32
BF16 = mybir.dt.bfloat16
AF = mybir.ActivationFunctionType
ALU = mybir.AluOpType

NTOK = 4096
NHALO = 2048
NKV = NTOK + NHALO
D_MODEL = 1024
D_FF = 2816
NFF = D_FF // 128
EPS = 1e-6

COMPUTE = ("pe", "act", "dve", "pool")
DMAQ = ("sp", "gq")
NSEM_DMA = 8
class Sched:
    def __init__(self, nc, stack):
        self.nc = nc
        self.ops = []
        self.lastw = {}
        self.readers = {}
        self.emitted = 0
        self.sem = {}
        for e in COMPUTE:
            self.sem[e] = stack.enter_context(nc.semaphore("s_" + e))
        self.dsem = {q: [stack.enter_context(nc.semaphore(f"d_{q}{i}")) for i in range(NSEM_DMA)]
                     for q in DMAQ}
        self.sigcount = {e: 0 for e in COMPUTE}
        self.dcount = {q: 0 for q in DMAQ}
        self.waited = {}
        self.last_op = {}
        self._pending_barrier = {}
        self._dma_ids = {}

    @staticmethod
    def stream(eng):
        return "pool" if eng == "gq" else ("sp" if eng == "spw" else eng)

    def op(self, eng, fn, reads=(), writes=()):
        deps = set(self.barrier_deps_for(eng))
        for r in reads:
            if r in self.lastw:
                deps.add(self.lastw[r])
        for w in writes:
            if w in self.lastw:
                deps.add(self.lastw[w])
            for rd in self.readers.get(w, ()):
                deps.add(rd)
        if eng == "spw":
            for q in DMAQ:
                deps.update(self._dma_ids.get(q, [])[-NSEM_DMA:])
        oid = len(self.ops)
        deps.discard(oid)
        self.ops.append(dict(eng=eng, fn=fn, deps=deps, sig=None))
        for r in reads:
            self.readers.setdefault(r, []).append(oid)
        for w in writes:
            self.lastw[w] = oid
            self.readers[w] = []
        if fn is not None:
            self.last_op[self.stream(eng)] = oid
        if eng in DMAQ:
            self._dma_ids.setdefault(eng, []).append(oid)
        return oid

    def barrier_deps_for(self, eng):
        st = self.stream(eng)
        if st in self._pending_barrier:
            d = self._pending_barrier.pop(st)
            return d
        return ()

    def barrier(self):
        deps = set()
        for st, oid in self.last_op.items():
            deps.add(oid)
        for q in DMAQ:
            ids = self._dma_ids.setdefault(q, [])
            deps.update(ids[-NSEM_DMA:])
        self._pending_barrier = {st: set(deps) for st in ("pe", "act", "dve", "pool", "sp")}

    def flush(self, final=False):
        nc = self.nc
        ops = self.ops
        lo = self.emitted
        hi = len(ops)
        needed = set()
        for i in range(lo, hi):
            o = ops[i]
            for d in o["deps"]:
                if d < lo:
                    continue
                de = ops[d]["eng"]
                if de in COMPUTE:
                    if de == "pe" and o["eng"] == "pe":
                        continue
                    needed.add(d)
        for d in needed:
            assert d >= lo or ops[d]["sig"] is not None, "cross-flush dep on unsignalled op"
        last_in = {}
        for i in range(lo, hi):
            if ops[i]["fn"] is not None:
                last_in[ops[i]["eng"]] = i
        for e, i in last_in.items():
            if e in COMPUTE:
                needed.add(i)
        for i in range(lo, hi):
            o = ops[i]
            e = o["eng"]
            if e in COMPUTE:
                if i in needed:
                    assert o["fn"] is not None
                    self.sigcount[e] += 1
                    o["sig"] = (self.sem[e], self.sigcount[e], ("c", e))
            elif e == "spw":
                pass
            else:
                k = self.dcount[e]
                self.dcount[e] += 1
                o["sig"] = (self.dsem[e][k % NSEM_DMA], 16 * (k // NSEM_DMA + 1), ("d", e, k % NSEM_DMA))
                o["dk"] = k
        per_stream = {st: [] for st in ("pe", "act", "dve", "pool", "sp")}
        for i in range(lo, hi):
            per_stream[self.stream(ops[i]["eng"])].append(i)

        def emit(st, engobj):
            for i in per_stream[st]:
                o = ops[i]
                waits = []
                for d in sorted(o["deps"]):
                    od = ops[d]
                    if od["eng"] == "pe" and o["eng"] == "pe":
                        continue
                    if od["sig"] is None:
                        assert d < lo, (d, lo, od["eng"], o["eng"])
                        continue
                    sem, val, key = od["sig"]
                    waits.append((sem, val, key))
                if o["eng"] in DMAQ:
                    k = o["dk"]
                    if k >= NSEM_DMA:
                        q = o["eng"]
                        waits.append((self.dsem[q][k % NSEM_DMA], 16 * (k // NSEM_DMA), ("d", q, k % NSEM_DMA)))
                for sem, val, key in waits:
                    wk = (st, key)
                    if self.waited.get(wk, 0) >= val:
                        continue
                    self.waited[wk] = val
                    engobj.wait_ge(sem, val)
                if o["fn"] is None:
                    continue
                ins = o["fn"](engobj)
                if o["sig"] is not None and o["eng"] in COMPUTE:
                    ins.then_inc(o["sig"][0], 1)
                elif o["eng"] in DMAQ:
                    ins.then_inc(o["sig"][0], 16)

        with nc.Block() as block:
            @block.sync
            def _(e):
                emit("sp", e)

            @block.scalar
            def _(e):
                emit("act", e)

            @block.vector
            def _(e):
                emit("dve", e)

            @block.gpsimd
            def _(e):
                emit("pool", e)

            @block.tensor
            def _(e):
                emit("pe", e)
        self.emitted = hi
        self.barrier()


def build_nc(debug=False, phases="ABCD"):
    nc = bass.Bass("TRN2", target_bir_lowering=False)

    def din(name, shape, dt=F32):
        return nc.dram_tensor(name, list(shape), dt, kind="ExternalInput").ap()

    xc = din("xc", [NKV, D_MODEL])
    pc = din("pc", [NTOK, 256])
    flag = din("flag", [128, 64])
    w_in = din("w_in", [1024, 2560])
    w_out = din("w_out", [1024, 1024])
    w_gu = din("w_gu", [1024, 2 * D_FF])
    w_dn = din("w_dn", [D_FF, 1024])
    w_pg = din("w_pg", [1024, 1024])
    w_pp = din("w_pp", [256, 1024])
    wspT_in = din("wspT", [128, 4, 128])
    bsT_in = din("bsT", [128, 4])
    g_pre_in = din("g_pre", [128, 8])
    g_out_in = din("g_out", [128, 8])
    g_ffn_in = din("g_ffn", [128, 8])
    lng_in = din("lng_bc", [128, 512])
    lnb_in = din("lnb_bc", [128, 512])
    gpm_in = din("gpm_bc", [128, 1024])
    gpf_in = din("gpf_bc", [128, 1024])
    bpe_in = din("bpe_bc", [128, 1024])

    skind = "ExternalOutput" if debug else "Internal"
    out = nc.dram_tensor("out", [NTOK, D_MODEL], F32, kind="ExternalOutput").ap()
    qT_d = nc.dram_tensor("qT_d", [4, 128, NTOK], BF16, kind=skind).ap()
    kT_d = nc.dram_tensor("kT_d", [4, 128, NKV], BF16, kind=skind).ap()
    v_d = nc.dram_tensor("v_d", [NKV, 512], BF16, kind=skind).ap()
    h1_d = nc.dram_tensor("h1_d", [NTOK, D_MODEL], F32, kind=skind).ap()
    gs_d = nc.dram_tensor("gs_d", [128, 4, NTOK], BF16, kind=skind).ap() if debug else None

    with ExitStack() as top:
        S = Sched(nc, top)

        def alloc(st, name, shape, dt):
            return st.enter_context(nc.sbuf_tensor(name + "_sb", list(shape), dt))

        def palloc(st, name, shape, dt):
            return st.enter_context(nc.psum_tensor(name + "_ps", list(shape), dt))

        ident = alloc(top, "ident", [128, 128], BF16)
        ones_bf = alloc(top, "ones_bf", [128, 128], BF16)
        flag_bf = alloc(top, "flag_bf", [128, 64], BF16)
        gstack = ExitStack()
        gsgu = alloc(gstack, "gsgu", [128, 4, NTOK], BF16)
        with ExitStack() as st:
            tmpf = alloc(st, "c_tmpf", [128, 128], F32)
            flg = alloc(st, "c_flg", [128, 64], F32)
            S.op("pool", lambda e: e.memset(tmpf[:], 1.0), writes=["tmpf"])
            S.op("dve", lambda e: e.tensor_copy(out=ones_bf[:], in_=tmpf[:]), reads=["tmpf"], writes=["ones_bf"])
            S.op("pool", lambda e: e.affine_select(out=tmpf[:], in_=tmpf[:], pattern=[[1, 128]],
                                                   compare_op=ALU.is_equal, fill=0.0, base=0,
                                                   channel_multiplier=-1), reads=["tmpf"], writes=["tmpf"])
            S.op("dve", lambda e: e.tensor_copy(out=ident[:], in_=tmpf[:]), reads=["tmpf"], writes=["ident"])
            S.op("sp", lambda e: e.dma_start(out=flg[:], in_=flag[:, :]), writes=["flg"])
            S.op("dve", lambda e: e.tensor_copy(out=flag_bf[:], in_=flg[:]), reads=["flg"], writes=["flag_bf"])
            S.flush()

        if "A" in phases:
          with ExitStack() as st:
            w_in_bf = alloc(st, "w_in_bf", [128, 8, 2560], BF16)
            wstage = [alloc(st, f"wstage{i}", [128, 2560], F32) for i in range(2)]
            g_pre = alloc(st, "g_pre_sb", [128, 8], F32)
            wsp_f = alloc(st, "wsp_f", [128, 4, 128], F32)
            wspT = alloc(st, "wspT", [128, 4, 128], BF16)
            bsT = alloc(st, "bsT", [128, 4], F32)
            lng = alloc(st, "lng", [128, 512], F32)
            lnb = alloc(st, "lnb", [128, 512], F32)
            xs = [alloc(st, f"xs{i}", [128, 4, 1024], F32) for i in range(2)]
            junk = alloc(st, "junkA", [128, 1024], BF16)
            ssq = [alloc(st, f"ssqA{i}", [128, 4], F32) for i in range(2)]
            rsq = [alloc(st, f"rsqA{i}", [128, 4], F32) for i in range(2)]
            ab = [alloc(st, f"abA{i}", [128, 1024], BF16) for i in range(2)]
            aT = [alloc(st, f"aTA{i}", [128, 8, 512], BF16) for i in range(2)]
            kqst = [alloc(st, f"kqst{i}", [128, 512], BF16) for i in range(4)]
            vst = [alloc(st, f"vst{i}", [128, 512], BF16) for i in range(2)]
            guz = alloc(st, "guz", [128, 1024], F32)
            st6 = alloc(st, "st6", [128, 4, 6], F32)
            mv = alloc(st, "mv", [128, 4, 2], F32)
            rstd4 = alloc(st, "rstd4", [128, 4], F32)
            nb4 = alloc(st, "nb4", [128, 4], F32)
            zn = alloc(st, "zn", [128, 512], F32)
            znb = alloc(st, "znb", [128, 512], BF16)
            sg = alloc(st, "sg", [128, 512], F32)
            ss2 = alloc(st, "ss2", [128, 1], F32)
            rs2 = alloc(st, "rs2", [128, 1], F32)
            sgn = alloc(st, "sgn", [128, 512], BF16)
            tp = [palloc(st, f"tpA{i}", [128, 8, 128], BF16) for i in range(2)]
            kq = [palloc(st, f"kqA{i}", [128, 512], F32) for i in range(2)]
            uz = palloc(st, "uzA", [128, 1024], F32)
            vp = palloc(st, "vpA", [128, 512], F32)
            mx = palloc(st, "mxA", [128, 512], F32)

            S.op("sp", lambda e: e.dma_start(out=g_pre[:], in_=g_pre_in[:, :]), writes=["g_pre"])
            S.op("sp", lambda e: e.dma_start(out=wsp_f[:], in_=wspT_in[:, :, :]), writes=["wsp_f"])
            S.op("sp", lambda e: e.dma_start(out=bsT[:], in_=bsT_in[:, :]), writes=["bsT"])
            S.op("sp", lambda e: e.dma_start(out=lng[:], in_=lng_in[:, :]), writes=["lng"])
            S.op("sp", lambda e: e.dma_start(out=lnb[:], in_=lnb_in[:, :]), writes=["lnb"])
            S.op("pool", lambda e: e.affine_select(out=wsp_f[:], in_=wsp_f[:], pattern=[[0, 4], [1, 128]],
                                                   compare_op=ALU.is_ge, fill=0.0, base=0,
                                                   channel_multiplier=-1), reads=["wsp_f"], writes=["wsp_f"])
            S.op("dve", lambda e: e.tensor_copy(out=wspT[:], in_=wsp_f[:]), reads=["wsp_f"], writes=["wspT"])
            for k in range(8):
                s = k % 2
                S.op("sp", lambda e, k=k, s=s: e.dma_start(out=wstage[s][:], in_=w_in[k * 128:(k + 1) * 128, :]),
                     writes=[("wstage", s)])
                if k % 2 == 0:
                    S.op("act", lambda e, k=k, s=s: e.activation(out=w_in_bf[:, k, :], in_=wstage[s][:], func=AF.Identity,
                                                                 scale=g_pre[:, k:k + 1]),
                         reads=[("wstage", s), "g_pre"], writes=[("w_in_bf", k)])
                else:
                    S.op("dve", lambda e, k=k, s=s: e.tensor_scalar(out=w_in_bf[:, k, :], in0=wstage[s][:],
                                                                    scalar1=g_pre[:, k:k + 1], scalar2=None, op0=ALU.mult),
                         reads=[("wstage", s), "g_pre"], writes=[("w_in_bf", k)])
            WIN = [("w_in_bf", k) for k in range(8)]

            nkq = 0
            nv = 0
            for bi in range(12):
                s = bi % 2
                main = bi >= 4
                S.op("sp", lambda e, bi=bi, s=s: e.dma_start(
                    out=xs[s][:], in_=xc[bi * 512:(bi + 1) * 512, :].rearrange("(t p) d -> p t d", p=128)),
                    writes=[("xs", s)])
                for t in range(4):
                    S.op("act", lambda e, s=s, t=t: e.activation(out=junk[:], in_=xs[s][:, t, :], func=AF.Square,
                                                                 accum_out=ssq[s][:, t:t + 1]),
                         reads=[("xs", s)], writes=["junkA", ("ssq", s, t)])
                S.op("act", lambda e, s=s: e.activation(out=rsq[s][:], in_=ssq[s][:], func=AF.Ln, scale=1.0 / 1024, bias=EPS),
                     reads=[("ssq", s, t) for t in range(4)], writes=[("rsq", s)])
                S.op("act", lambda e, s=s: e.activation(out=rsq[s][:], in_=rsq[s][:], func=AF.Exp, scale=-0.5),
                     reads=[("rsq", s)], writes=[("rsq", s)])
                for t in range(4):
                    a = t % 2
                    S.op("dve", lambda e, s=s, t=t, a=a: e.tensor_scalar(out=ab[a][:], in0=xs[s][:, t, :],
                                                                         scalar1=rsq[s][:, t:t + 1], scalar2=None,
                                                                         op0=ALU.mult),
                         reads=[("xs", s), ("rsq", s)], writes=[("ab", a)])
                    for k in range(8):
                        S.op("pe", lambda e, a=a, k=k: e.transpose(out=tp[a][:, k, :], in_=ab[a][:, k * 128:(k + 1) * 128],
                                                                   identity=ident[:]),
                             reads=[("ab", a), "ident"], writes=[("tp", a)])
                    if t % 2 == 0:
                        S.op("act", lambda e, s=s, t=t, a=a: e.copy(out=aT[s][:, :, t * 128:(t + 1) * 128], in_=tp[a][:]),
                             reads=[("tp", a)], writes=[("aT", s, t)])
                    else:
                        S.op("dve", lambda e, s=s, t=t, a=a: e.tensor_copy(out=aT[s][:, :, t * 128:(t + 1) * 128], in_=tp[a][:]),
                             reads=[("tp", a)], writes=[("aT", s, t)])
                AT = [("aT", s, t) for t in range(4)]
                for which in (("k", "q") if main else ("k",)):
                    for c in range(4):
                        pz = nkq % 2
                        ks_ = nkq % 4
                        nkq += 1
                        col0 = (512 if which == "k" else 0) + c * 128
                        for k in range(8):
                            S.op("pe", lambda e, pz=pz, k=k, col0=col0, s=s: e.matmul(
                                kq[pz][:], lhsT=w_in_bf[:, k, col0:col0 + 128], rhs=aT[s][:, k, :],
                                start=(k == 0), stop=(k == 7)),
                                reads=AT + WIN, writes=[("kq", pz)])
                        if which == "k":
                            S.op("act", lambda e, pz=pz, ks_=ks_: e.copy(out=kqst[ks_][:], in_=kq[pz][:]),
                                 reads=[("kq", pz)], writes=[("kqst", ks_)])
                            S.op("gq", lambda e, ks_=ks_, c=c, bi=bi: e.dma_start(
                                out=kT_d[c, :, bi * 512:(bi + 1) * 512], in_=kqst[ks_][:]),
                                reads=[("kqst", ks_)], writes=["kT_d"])
                        else:
                            S.op("dve", lambda e, pz=pz, ks_=ks_: e.tensor_scalar(out=kqst[ks_][:], in0=kq[pz][:],
                                                                                  scalar1=0.125, scalar2=None,
                                                                                  op0=ALU.mult),
                                 reads=[("kq", pz)], writes=[("kqst", ks_)])
                            S.op("gq", lambda e, ks_=ks_, c=c, bi=bi: e.dma_start(
                                out=qT_d[c, :, (bi - 4) * 512:(bi - 3) * 512], in_=kqst[ks_][:]),
                                reads=[("kqst", ks_)], writes=["qT_d"])
                for t in range(4):
                    vs_ = nv % 2
                    nv += 1
                    for k in range(8):
                        S.op("pe", lambda e, k=k, s=s, t=t: e.matmul(
                            vp[:], lhsT=aT[s][:, k, t * 128:(t + 1) * 128], rhs=w_in_bf[:, k, 1024:1536],
                            start=(k == 0), stop=(k == 7)),
                            reads=AT + WIN, writes=["vp"])
                    S.op("act", lambda e, vs_=vs_: e.copy(out=vst[vs_][:], in_=vp[:]), reads=["vp"], writes=[("vst", vs_)])
                    S.op("gq", lambda e, vs_=vs_, bi=bi, t=t: e.dma_start(
                        out=v_d[bi * 512 + t * 128: bi * 512 + (t + 1) * 128, :], in_=vst[vs_][:]),
                        reads=[("vst", vs_)], writes=["v_d"])
                    if not main:
                        continue
                    tok = (bi - 4) * 512 + t * 128
                    for hf in range(2):
                        for k in range(8):
                            S.op("pe", lambda e, k=k, s=s, t=t, hf=hf: e.matmul(
                                uz[:, hf * 512:(hf + 1) * 512], lhsT=aT[s][:, k, t * 128:(t + 1) * 128],
                                rhs=w_in_bf[:, k, 1536 + hf * 512:2048 + hf * 512], start=(k == 0), stop=(k == 7)),
                                reads=AT + WIN, writes=["uz"])
                    S.op("act", lambda e: e.activation(out=guz[:], in_=uz[:], func=AF.Gelu_apprx_tanh),
                         reads=["uz"], writes=["guz"])
                    for g in range(4):
                        S.op("dve", lambda e, g=g: e.bn_stats(out=st6[:, g, :], in_=guz[:, 512 + g * 128:512 + (g + 1) * 128]),
                             reads=["guz"], writes=[("st6", g)])
                    for g in range(4):
                        S.op("dve", lambda e, g=g: e.bn_aggr(out=mv[:, g, :], in_=st6[:, g, :]),
                             reads=[("st6", g)], writes=[("mv", g)])
                    MV = [("mv", g) for g in range(4)]
                    S.op("act", lambda e: e.activation(out=rstd4[:], in_=mv[:, :, 1], func=AF.Ln, bias=EPS),
                         reads=MV, writes=["rstd4"])
                    S.op("act", lambda e: e.activation(out=rstd4[:], in_=rstd4[:], func=AF.Exp, scale=-0.5),
                         reads=["rstd4"], writes=["rstd4"])
                    S.op("dve", lambda e: e.scalar_tensor_tensor(out=nb4[:], in0=mv[:, :, 0], scalar=-1.0, in1=rstd4[:],
                                                                 op0=ALU.mult, op1=ALU.mult),
                         reads=MV + ["rstd4"], writes=["nb4"])
                    for g in range(4):
                        S.op("dve", lambda e, g=g: e.tensor_scalar(out=zn[:, g * 128:(g + 1) * 128],
                                                                   in0=guz[:, 512 + g * 128:512 + (g + 1) * 128],
                                                                   scalar1=rstd4[:, g:g + 1], scalar2=nb4[:, g:g + 1],
                                                                   op0=ALU.mult, op1=ALU.add),
                             reads=["guz", "rstd4", "nb4"], writes=[("zn", g)])
                    ZN = [("zn", g) for g in range(4)]
                    S.op("dve", lambda e: e.tensor_tensor(out=zn[:], in0=zn[:], in1=lng[:], op=ALU.mult),
                         reads=ZN + ["lng"], writes=ZN)
                    S.op("dve", lambda e: e.tensor_tensor(out=znb[:], in0=zn[:], in1=lnb[:], op=ALU.add),
                         reads=ZN + ["lnb"], writes=["znb"])
                    for g in range(4):
                        S.op("pe", lambda e, g=g: e.matmul(mx[:, g * 128:(g + 1) * 128], lhsT=wspT[:, g, :],
                                                           rhs=znb[:, g * 128:(g + 1) * 128], start=True, stop=True),
                             reads=["znb", "wspT"], writes=["mx"])
                    for g in range(4):
                        S.op("dve", lambda e, g=g: e.scalar_tensor_tensor(
                            out=sg[:, g * 128:(g + 1) * 128], in0=mx[:, g * 128:(g + 1) * 128], scalar=bsT[:, g:g + 1],
                            in1=guz[:, g * 128:(g + 1) * 128], op0=ALU.add, op1=ALU.mult),
                            reads=["mx", "guz", "bsT"], writes=[("sg", g)])
                    SG = [("sg", g) for g in range(4)]
                    S.op("act", lambda e: e.activation(out=junk[:, 0:512], in_=sg[:], func=AF.Square, accum_out=ss2[:]),
                         reads=SG, writes=["junkA", "ss2"])
                    S.op("act", lambda e: e.activation(out=rs2[:], in_=ss2[:], func=AF.Ln, scale=1.0 / 512, bias=EPS),
                         reads=["ss2"], writes=["rs2"])
                    S.op("act", lambda e: e.activation(out=rs2[:], in_=rs2[:], func=AF.Exp, scale=-0.5),
                         reads=["rs2"], writes=["rs2"])
                    S.op("dve", lambda e: e.tensor_scalar(out=sgn[:], in0=sg[:], scalar1=rs2[:, 0:1], scalar2=None,
                                                          op0=ALU.mult),
                         reads=SG + ["rs2"], writes=["sgn"])
                    a2 = t % 2
                    for g in range(4):
                        S.op("pe", lambda e, g=g, a2=a2: e.transpose(out=tp[a2][:, g, :], in_=sgn[:, g * 128:(g + 1) * 128],
                                                                     identity=ident[:]),
                             reads=["sgn", "ident"], writes=[("tp", a2)])
                    S.op("dve", lambda e, a2=a2, tok=tok: e.tensor_copy(out=gsgu[:, :, tok:tok + 128], in_=tp[a2][:, 0:4, :]),
                         reads=[("tp", a2)], writes=[("gsgu", tok // 128)])
            if debug:
                S.op("sp", lambda e: e.dma_start(out=gs_d[:, :, :], in_=gsgu[:]),
                     reads=[("gsgu", i) for i in range(32)], writes=["gs_d"])
            S.flush()

        if "B" in phases:
          with ExitStack() as st:
            w_out_bf = alloc(st, "w_out_bf", [128, 8, 1024], BF16)
            g_out = alloc(st, "g_outB", [128, 8], F32)
            gpm = alloc(st, "gpmB", [128, 1024], F32)
            Mk = [[alloc(st, f"Mk{j}_{di}", [128, 2, 2, 128], BF16) for di in range(3)] for j in range(4)]
            iot = alloc(st, "iotB", [128, 2, 128], F32)
            mtmp = alloc(st, "mtmpB", [128, 2, 128], F32)
            qs = [alloc(st, f"qsB{i}", [128, 2048], BF16) for i in range(2)]
            ks = [alloc(st, f"ksB{i}", [128, 4096], BF16) for i in range(2)]
            V1 = [alloc(st, f"V1B{i}", [128, 17, 128], BF16) for i in range(2)]
            V4 = [alloc(st, f"V4B{i}", [128, 5, 4, 128], BF16) for i in range(2)]
            V16 = [alloc(st, f"V16B{i}", [128, 2, 16, 128], BF16) for i in range(2)]
            acc = alloc(st, "accB", [128, 2, 2048], F32)
            wstg = [acc[:, i, 0:1024] for i in range(2)]
            outT = alloc(st, "outTB", [128, 4, 2048], F32)
            Eb = [alloc(st, f"EbB{i}", [128, 512], BF16) for i in range(2)]
            Em = [alloc(st, f"EmB{i}", [128, 2, 2, 128], BF16) for i in range(3)]
            sq = alloc(st, "sqB", [128, 4, 512], BF16)
            rb = alloc(st, "rbB", [128, 512], F32)
            gTa = alloc(st, "gTaB", [128, 4, 512], BF16)
            xt = [alloc(st, f"xtB{i}", [128, 1024], F32) for i in range(2)]
            junkB = alloc(st, "junkB", [128, 1024], BF16)
            ss1 = alloc(st, "ss1B", [128, 1], F32)
            rs1 = alloc(st, "rs1B", [128, 1], F32)
            t1 = alloc(st, "t1B", [128, 1024], F32)
            sc = [palloc(st, f"scB{i}", [128, 2, 512], F32) for i in range(2)]
            pvb = palloc(st, "pvB", [128, 2, 2, 128], F32)
            pv = [pvb[:, i, :, :] for i in range(2)]
            ssp = palloc(st, "sspB", [128, 512], F32)
            mixed = palloc(st, "mixedB", [128, 1024], F32)

            S.op("sp", lambda e: e.dma_start(out=g_out[:], in_=g_out_in[:, :]), writes=["g_out"])
            S.op("sp", lambda e: e.dma_start(out=gpm[:], in_=gpm_in[:, :]), writes=["gpm"])
            for k in range(8):
                s = k % 2
                S.op("sp", lambda e, k=k, s=s: e.dma_start(out=wstg[s], in_=w_out[k * 128:(k + 1) * 128, :]),
                     writes=[("wstg", s)])
                S.op("dve", lambda e, k=k, s=s: e.tensor_scalar(out=w_out_bf[:, k, :], in0=wstg[s],
                                                                scalar1=g_out[:, k:k + 1], scalar2=None, op0=ALU.mult),
                     reads=[("wstg", s), "g_out"], writes=[("w_out_bf", k)])
            WOUT = [("w_out_bf", k) for k in range(8)]
            S.op("pool", lambda e: e.iota(iot[:, 0, :], pattern=[[1, 128]], base=128, channel_multiplier=-1,
                                          allow_small_or_imprecise_dtypes=True), writes=["iot0"])
            S.op("pool", lambda e: e.iota(iot[:, 1, :], pattern=[[1, 128]], base=0, channel_multiplier=-1,
                                          allow_small_or_imprecise_dtypes=True), writes=["iot1"])
            S.op("dve", lambda e: e.tensor_scalar(out=iot[:], in0=iot[:], scalar1=0.0, scalar2=None, op0=ALU.max),
                 reads=["iot0", "iot1"], writes=["iot"])
            for j in range(4):
                for di, D in enumerate((1, 4, 16)):
                    for hl in range(2):
                        h = 2 * j + hl
                        slope = 2.0 ** (-(h + 1))
                        S.op("act", lambda e, slope=slope, D=D: e.activation(out=mtmp[:], in_=iot[:], func=AF.Exp,
                                                                             scale=-slope * D),
                             reads=["iot"], writes=["mtmp"])
                        S.op("pool", lambda e, j=j, di=di, hl=hl: e.affine_select(
                            out=Mk[j][di][:, hl, 0, :], in_=mtmp[:, 0, :], pattern=[[-1, 128]], compare_op=ALU.is_ge,
                            fill=0.0, base=0, channel_multiplier=1), reads=["mtmp"], writes=[("Mk", j, di, hl, 0)])
                        S.op("pool", lambda e, j=j, di=di, hl=hl: e.affine_select(
                            out=Mk[j][di][:, hl, 1, :], in_=mtmp[:, 1, :], pattern=[[1, 128]], compare_op=ALU.is_ge,
                            fill=0.0, base=0, channel_multiplier=-1), reads=["mtmp"], writes=[("Mk", j, di, hl, 1)])

            def accres(D, nd, r):
                if D == 16:
                    return [("acc", t, r) for t in range(16)]
                if D == 4:
                    return [("acc", nd * 4 + t, r + 4 * m) for t in range(4) for m in range(4)]
                return [("acc", nd, m) for m in range(16)]

            ALLACC = [("acc", t, m) for t in range(16) for m in range(16)]
            nblk = 0
            nit = 0
            nx = 0
            import os
            NSPAN = int(os.environ.get("NSPAN", "2")); NJ = int(os.environ.get("NJ", "4")); NDIL = int(os.environ.get("NDIL", "3")); NND = int(os.environ.get("NND", "99")); NOC = int(os.environ.get("NOC", "0"))
            for n in range(NSPAN):
                S0 = NHALO + n * 2048
                for j in range(NJ):
                    sl = nit % 2
                    nit += 1
                    jc = slice(j * 128, (j + 1) * 128)
                    S.op("sp", lambda e, sl=sl, j=j, n=n: e.dma_start(out=qs[sl][:], in_=qT_d[j, :, n * 2048:(n + 1) * 2048]),
                         reads=["qT_d"], writes=[("qs", sl)])
                    S.op("sp", lambda e, sl=sl, j=j, S0=S0: e.dma_start(out=ks[sl][:], in_=kT_d[j, :, S0 - 2048:S0 + 2048]),
                         reads=["kT_d"], writes=[("ks", sl)])
                    for t0 in range(0, 17, 5):
                        t1_ = min(17, t0 + 5)
                        S.op("sp", lambda e, sl=sl, jc=jc, S0=S0, t0=t0, t1_=t1_: e.dma_start(
                            out=V1[sl][:, t0:t1_, :],
                            in_=v_d[S0 - 128 + t0 * 128:S0 - 128 + t1_ * 128, jc].rearrange("(t p) c -> p t c", p=128)),
                            reads=["v_d"], writes=[("V1", sl, t0)])
                    for n4 in range(5):
                        S.op("sp", lambda e, sl=sl, jc=jc, S0=S0, n4=n4: e.dma_start(
                            out=V4[sl][:, n4, :, :],
                            in_=v_d[S0 - 512 + n4 * 512:S0 + n4 * 512, jc].rearrange("(p r) c -> p r c", r=4)),
                            reads=["v_d"], writes=[("V4", sl, n4)])
                    for n16 in range(2):
                        for r0 in range(0, 16, 4):
                            S.op("sp", lambda e, sl=sl, jc=jc, S0=S0, n16=n16, r0=r0: e.dma_start(
                                out=V16[sl][:, n16, r0:r0 + 4, :],
                                in_=v_d[S0 - 2048 + n16 * 2048:S0 + n16 * 2048, jc].rearrange(
                                    "(p r) c -> p r c", r=16)[:, r0:r0 + 4, :]),
                                reads=["v_d"], writes=[("V16", sl, n16, r0)])
                    LOADS = ([("qs", sl), ("ks", sl)] + [("V1", sl, t0) for t0 in range(0, 17, 5)]
                             + [("V4", sl, n4) for n4 in range(5)]
                             + [("V16", sl, a, b) for a in range(2) for b in range(0, 16, 4)])
                    for di, D in ((2, 16), (1, 4), (0, 1))[:NDIL]:
                        for nd in range(min(16 // D, NND)):
                            for r in range(D):
                                base = nd * 128 * D + r
                                qsl = slice(base, base + 127 * D + 1, D)
                                kp = slice(2048 + base - 128 * D, 2048 + base - 128 * D + 127 * D + 1, D)
                                kc = slice(2048 + base, 2048 + base + 127 * D + 1, D)
                                if D == 16:
                                    Vp, Vc = V16[sl][:, 0, r, :], V16[sl][:, 1, r, :]
                                elif D == 4:
                                    Vp, Vc = V4[sl][:, nd, r, :], V4[sl][:, nd + 1, r, :]
                                else:
                                    Vp, Vc = V1[sl][:, nd, :], V1[sl][:, nd + 1, :]
                                halo_prev = (n == 0 and nd == 0)
                                b2 = nblk % 2
                                b3 = nblk % 3
                                nblk += 1
                                for hl in range(2):
                                    hp = slice(hl * 64, (hl + 1) * 64)
                                    for kb, ksl in ((0, kp), (1, kc)):
                                        S.op("pe", lambda e, b2=b2, hl=hl, kb=kb, hp=hp, ksl=ksl, qsl=qsl, sl=sl: e.matmul(
                                            sc[b2][:, hl, kb * 128:(kb + 1) * 128], lhsT=ks[sl][hp, ksl], rhs=qs[sl][hp, qsl],
                                            start=True, stop=True),
                                            reads=LOADS, writes=[("sc", b2)])
                                S.op("act", lambda e, b2=b2: e.activation(out=Eb[b2][:].rearrange("p (h c) -> p h c", h=2), in_=sc[b2][:, :, 0:256], func=AF.Exp),
                                     reads=[("sc", b2)], writes=[("Eb", b2)])
                                S.op("dve", lambda e, b2=b2, b3=b3, j=j, di=di: e.tensor_tensor(
                                    out=Em[b3][:], in0=Eb[b2][:].rearrange("p (h k q) -> p h k q", h=2, k=2), in1=Mk[j][di][:], op=ALU.mult),
                                    reads=[("Eb", b2)] + [("Mk", j, di, a, b) for a in range(2) for b in range(2)],
                                    writes=[("Em", b3)])
                                for hl in range(2):
                                    hp = slice(hl * 64, (hl + 1) * 64)
                                    onesp = flag_bf[:, :] if halo_prev else ones_bf[:, 0:64]
                                    S.op("pe", lambda e, b2=b2, b3=b3, hl=hl, hp=hp, Vp=Vp: e.matmul(
                                        pv[b2][hp, 0, :], lhsT=Vp[:, hp], rhs=Em[b3][:, hl, 0, :], start=True, stop=False),
                                        reads=LOADS + [("Em", b3)], writes=[("pv", b2)])
                                    S.op("pe", lambda e, b2=b2, b3=b3, hl=hl, hp=hp, Vc=Vc: e.matmul(
                                        pv[b2][hp, 0, :], lhsT=Vc[:, hp], rhs=Em[b3][:, hl, 1, :], start=False, stop=True),
                                        reads=LOADS + [("Em", b3)], writes=[("pv", b2)])
                                    S.op("pe", lambda e, b2=b2, b3=b3, hl=hl, hp=hp, onesp=onesp: e.matmul(
                                        pv[b2][hp, 1, :], lhsT=onesp, rhs=Em[b3][:, hl, 0, :], start=True, stop=False),
                                        reads=[("Em", b3), "flag_bf", "ones_bf"], writes=[("pv", b2)])
                                    S.op("pe", lambda e, b2=b2, b3=b3, hl=hl, hp=hp: e.matmul(
                                        pv[b2][hp, 1, :], lhsT=ones_bf[:, 0:64], rhs=Em[b3][:, hl, 1, :], start=False, stop=True),
                                        reads=[("Em", b3), "ones_bf"], writes=[("pv", b2)])
                                AR = accres(D, nd, r)
                                if D == 16:
                                    S.op("act", lambda e, b2=b2, qsl=qsl: e.copy(out=acc[:, :, qsl], in_=pv[b2]),
                                         reads=[("pv", b2)], writes=AR)
                                else:
                                    S.op("dve", lambda e, b2=b2, qsl=qsl: e.tensor_tensor(
                                        out=acc[:, :, qsl], in0=pv[b2], in1=acc[:, :, qsl], op=ALU.add),
                                        reads=[("pv", b2)] + AR, writes=AR)
                    S.op("act", lambda e: e.activation(out=acc[:, 1, :], in_=acc[:, 1, :], func=AF.Ln), reads=ALLACC, writes=ALLACC)
                    S.op("act", lambda e: e.activation(out=acc[:, 1, :], in_=acc[:, 1, :], func=AF.Exp, scale=-1.0),
                         reads=ALLACC, writes=ALLACC)
                    S.op("dve", lambda e, j=j: e.tensor_tensor(out=outT[:, j, :], in0=acc[:, 0, :], in1=acc[:, 1, :], op=ALU.mult),
                         reads=ALLACC, writes=[("outT", j)])
                OUTT = [("outT", j) for j in range(4)]
                for b in range(0 if NOC else 4):
                    cs = slice(b * 512, (b + 1) * 512)
                    for j in range(4):
                        S.op("act", lambda e, j=j, cs=cs: e.activation(out=sq[:, j, :], in_=outT[:, j, cs], func=AF.Square),
                             reads=OUTT, writes=[("sq", j)])
                    for j in range(4):
                        S.op("pe", lambda e, j=j: e.matmul(ssp[:], lhsT=ones_bf[:], rhs=sq[:, j, :], start=(j == 0), stop=(j == 3)),
                             reads=[("sq", j), "ones_bf"], writes=["ssp"])
                    S.op("act", lambda e: e.activation(out=rb[:], in_=ssp[:], func=AF.Ln, scale=1.0 / 512, bias=EPS),
                         reads=["ssp"], writes=["rb"])
                    S.op("act", lambda e: e.activation(out=rb[:], in_=rb[:], func=AF.Exp, scale=-0.5), reads=["rb"], writes=["rb"])
                    for j in range(4):
                        S.op("dve", lambda e, j=j, cs=cs: e.tensor_tensor(out=gTa[:, j, :], in0=outT[:, j, cs], in1=rb[:], op=ALU.mult),
                             reads=OUTT + ["rb"], writes=[("gTa", j)])
                    GTA = [("gTa", j) for j in range(4)]
                    for t in range(4):
                        tok = n * 2048 + b * 512 + t * 128
                        xsl = nx % 2
                        nx += 1
                        S.op("sp", lambda e, xsl=xsl, tok=tok: e.dma_start(out=xt[xsl][:], in_=xc[NHALO + tok:NHALO + tok + 128, :]),
                             writes=[("xt", xsl)])
                        for hf in range(2):
                            for k in range(8):
                                if k < 4:
                                    lh = gTa[:, k, t * 128:(t + 1) * 128]
                                else:
                                    lh = gsgu[:, k - 4, tok:tok + 128]
                                S.op("pe", lambda e, lh=lh, k=k, hf=hf: e.matmul(
                                    mixed[:, hf * 512:(hf + 1) * 512], lhsT=lh, rhs=w_out_bf[:, k, hf * 512:(hf + 1) * 512],
                                    start=(k == 0), stop=(k == 7)),
                                    reads=GTA + WOUT + [("gsgu", tok // 128)], writes=["mixed"])
                        S.op("act", lambda e: e.activation(out=junkB[:], in_=mixed[:], func=AF.Square, accum_out=ss1[:]),
                             reads=["mixed"], writes=["junkB", "ss1"])
                        S.op("act", lambda e: e.activation(out=rs1[:], in_=ss1[:], func=AF.Ln, scale=1.0 / 1024, bias=EPS),
                             reads=["ss1"], writes=["rs1"])
                        S.op("act", lambda e: e.activation(out=rs1[:], in_=rs1[:], func=AF.Exp, scale=-0.5),
                             reads=["rs1"], writes=["rs1"])
                        S.op("dve", lambda e: e.scalar_tensor_tensor(out=t1[:], in0=mixed[:], scalar=rs1[:, 0:1], in1=gpm[:],
                                                                     op0=ALU.mult, op1=ALU.mult),
                             reads=["mixed", "rs1", "gpm"], writes=["t1"])
                        S.op("pool", lambda e, xsl=xsl: e.tensor_tensor(out=xt[xsl][:], in0=t1[:], in1=xt[xsl][:], op=ALU.add),
                             reads=["t1", ("xt", xsl)], writes=[("xt", xsl)])
                        S.op("gq", lambda e, xsl=xsl, tok=tok: e.dma_start(out=h1_d[tok:tok + 128, :], in_=xt[xsl][:]),
                             reads=[("xt", xsl)], writes=["h1_d"])
            S.flush()

        gstack.close()
        if "D" in phases:
          with ExitStack() as st:
            wgu = alloc(st, "wgu_bf", [128, 8, 2 * D_FF], BF16)
            wdn = alloc(st, "wdn_bf", [128, NFF, 1024], BF16)
            wpg = alloc(st, "wpg_bf", [128, 8, 1024], BF16)
            wpp = alloc(st, "wpp_bf", [128, 2, 1024], BF16)
            g_ffn = alloc(st, "g_ffnD", [128, 8], F32)
            NB = 256
            with ExitStack() as st2:
                stg = [alloc(st2, f"stgD{i}", [128, 1408], F32) for i in range(3)]
                S.op("sp", lambda e: e.dma_start(out=g_ffn[:], in_=g_ffn_in[:, :]), writes=["g_ffn"])
                pieces = []
                for k in range(8):
                    for q4 in range(4):
                        pieces.append(("gu", k, q4))
                for c in range(NFF):
                    pieces.append(("dn", c, 0))
                for k in range(8):
                    pieces.append(("pg", k, 0))
                for k in range(2):
                    pieces.append(("pp", k, 0))
                for i, (kind, k, q4) in enumerate(pieces):
                    s = i % 3
                    if kind == "gu":
                        src = w_gu[k * 128:(k + 1) * 128, q4 * 1408:(q4 + 1) * 1408]
                        dst = wgu[:, k, q4 * 1408:(q4 + 1) * 1408]
                        w = 1408
                    elif kind == "dn":
                        src = w_dn[k * 128:(k + 1) * 128, :]
                        dst = wdn[:, k, :]
                        w = 1024
                    elif kind == "pg":
                        src = w_pg[k * 128:(k + 1) * 128, :]
                        dst = wpg[:, k, :]
                        w = 1024
                    else:
                        src = w_pp[k * 128:(k + 1) * 128, :]
                        dst = wpp[:, k, :]
                        w = 1024
                    S.op("sp", lambda e, s=s, src=src, w=w: e.dma_start(out=stg[s][:, 0:w], in_=src), writes=[("stg", s)])
                    eng = ("act", "dve", "pool")[i % 3]
                    if kind == "gu":
                        if eng == "act":
                            S.op("act", lambda e, s=s, dst=dst, w=w, k=k: e.activation(out=dst, in_=stg[s][:, 0:w], func=AF.Identity,
                                                                                       scale=g_ffn[:, k:k + 1]),
                                 reads=[("stg", s), "g_ffn"], writes=[("wD", i)])
                        else:
                            S.op(eng, lambda e, s=s, dst=dst, w=w, k=k: e.tensor_scalar(out=dst, in0=stg[s][:, 0:w],
                                                                                        scalar1=g_ffn[:, k:k + 1], scalar2=None,
                                                                                        op0=ALU.mult),
                                 reads=[("stg", s), "g_ffn"], writes=[("wD", i)])
                    else:
                        if eng == "act":
                            S.op("act", lambda e, s=s, dst=dst, w=w: e.copy(out=dst, in_=stg[s][:, 0:w]),
                                 reads=[("stg", s)], writes=[("wD", i)])
                        else:
                            S.op(eng, lambda e, s=s, dst=dst, w=w: e.tensor_copy(out=dst, in_=stg[s][:, 0:w]),
                                 reads=[("stg", s)], writes=[("wD", i)])
                S.flush()
            gpf = alloc(st, "gpfD", [128, 1024], F32)
            bpe = alloc(st, "bpeD", [128, 1024], F32)
            fT = alloc(st, "fTD", [128, 8, NB], BF16)
            pT = alloc(st, "pTD", [128, 2, NB], BF16)
            actT = alloc(st, "actTD", [128, NFF, NB], BF16)
            h2T = alloc(st, "h2TD", [128, 8, 128], BF16)
            h1t = [alloc(st, f"h1tD{i}", [128, 1024], F32) for i in range(2)]
            bufA = alloc(st, "bufAD", [128, 1024], F32)
            bufB = alloc(st, "bufBD", [128, 1024], F32)
            h2t = alloc(st, "h2tD", [128, 1024], F32)
            cb = alloc(st, "cbD", [128, 1024], BF16)
            pt = [alloc(st, f"ptD{i}", [128, 256], F32) for i in range(2)]
            pb = alloc(st, "pbD", [128, 256], BF16)
            slu = [alloc(st, f"sluD{i}", [128, NB], F32) for i in range(2)]
            ssD = alloc(st, "ssD", [128, 1], F32)
            rsD = alloc(st, "rsD", [128, 1], F32)
            tpD = palloc(st, "tpD", [128, 8, 128], BF16)
            gup = [palloc(st, f"gupD{i}", [128, 2, NB], F32) for i in range(3)]
            big = [palloc(st, f"bigD{i}", [128, 1024], F32) for i in range(2)]

            S.op("sp", lambda e: e.dma_start(out=gpf[:], in_=gpf_in[:, :]), writes=["gpf"])
            S.op("sp", lambda e: e.dma_start(out=bpe[:], in_=bpe_in[:, :]), writes=["bpe"])
            nh = 0
            ng = 0
            nbig = 0
            ncp = 0
            for blk in range(NTOK // NB):
                hslots = []
                for t in range(NB // 128):
                    tok = blk * NB + t * 128
                    hs = nh % 2
                    nh += 1
                    hslots.append(hs)
                    S.op("sp", lambda e, hs=hs, tok=tok: e.dma_start(out=h1t[hs][:], in_=h1_d[tok:tok + 128, :]),
                         reads=["h1_d"], writes=[("h1t", hs)])
                    S.op("sp", lambda e, hs=hs, tok=tok: e.dma_start(out=pt[hs][:], in_=pc[tok:tok + 128, :]),
                         writes=[("pt", hs)])
                    S.op("act", lambda e, hs=hs: e.activation(out=cb[:], in_=h1t[hs][:], func=AF.Square, accum_out=ssD[:]),
                         reads=[("h1t", hs)], writes=["cb", "ssD"])
                    S.op("act", lambda e: e.activation(out=rsD[:], in_=ssD[:], func=AF.Ln, scale=1.0 / 1024, bias=EPS),
                         reads=["ssD"], writes=["rsD"])
                    S.op("act", lambda e: e.activation(out=rsD[:], in_=rsD[:], func=AF.Exp, scale=-0.5), reads=["rsD"], writes=["rsD"])
                    S.op("dve", lambda e, hs=hs: e.tensor_scalar(out=cb[:], in0=h1t[hs][:], scalar1=rsD[:, 0:1], scalar2=None,
                                                                 op0=ALU.mult),
                         reads=[("h1t", hs), "rsD"], writes=["cb"])
                    for k in range(8):
                        S.op("pe", lambda e, k=k: e.transpose(out=tpD[:, k, :], in_=cb[:, k * 128:(k + 1) * 128], identity=ident[:]),
                             reads=["cb", "ident"], writes=["tpD"])
                    S.op("act", lambda e, t=t: e.copy(out=fT[:, :, t * 128:(t + 1) * 128], in_=tpD[:]),
                         reads=["tpD"], writes=[("fT", t)])
                    S.op("dve", lambda e, hs=hs: e.tensor_copy(out=pb[:], in_=pt[hs][:]), reads=[("pt", hs)], writes=["pb"])
                    for k in range(2):
                        S.op("pe", lambda e, k=k: e.transpose(out=tpD[:, k, :], in_=pb[:, k * 128:(k + 1) * 128], identity=ident[:]),
                             reads=["pb", "ident"], writes=["tpD"])
                    S.op("dve", lambda e, t=t: e.tensor_copy(out=pT[:, :, t * 128:(t + 1) * 128], in_=tpD[:, 0:2, :]),
                         reads=["tpD"], writes=[("pT", t)])
                FT = [("fT", t) for t in range(NB // 128)]
                WGU = [("wD", i) for i in range(32)]
                WDN = [("wD", 32 + i) for i in range(NFF)]
                WPG = [("wD", 32 + NFF + i) for i in range(8)]
                WPP = [("wD", 40 + NFF + i) for i in range(2)]
                for c in range(NFF):
                    gs_ = ng % 3
                    s2 = ng % 2
                    ng += 1
                    for half in range(2):
                        col0 = half * D_FF + c * 128
                        for k in range(8):
                            S.op("pe", lambda e, gs_=gs_, half=half, col0=col0, k=k: e.matmul(
                                gup[gs_][:, half, :], lhsT=wgu[:, k, col0:col0 + 128], rhs=fT[:, k, :],
                                start=(k == 0), stop=(k == 7)),
                                reads=FT + WGU, writes=[("gup", gs_)])
                    S.op("act", lambda e, gs_=gs_, s2=s2: e.activation(out=slu[s2][:], in_=gup[gs_][:, 0, :], func=AF.Silu),
                         reads=[("gup", gs_)], writes=[("slu", s2)])
                    S.op("dve", lambda e, gs_=gs_, s2=s2, c=c: e.tensor_tensor(out=actT[:, c, :], in0=gup[gs_][:, 1, :],
                                                                               in1=slu[s2][:], op=ALU.mult),
                         reads=[("gup", gs_), ("slu", s2)], writes=[("actT", c)])
                ACTT = [("actT", c) for c in range(NFF)]
                for t in range(NB // 128):
                    tok = blk * NB + t * 128
                    hs = hslots[t]
                    yb = nbig % 2
                    nbig += 1
                    for half in range(2):
                        for c in range(NFF):
                            S.op("pe", lambda e, yb=yb, half=half, c=c, t=t: e.matmul(
                                big[yb][:, half * 512:(half + 1) * 512], lhsT=actT[:, c, t * 128:(t + 1) * 128],
                                rhs=wdn[:, c, half * 512:(half + 1) * 512], start=(c == 0), stop=(c == NFF - 1)),
                                reads=ACTT + WDN, writes=[("big", yb)])
                    S.op("act", lambda e, yb=yb: e.activation(out=cb[:], in_=big[yb][:], func=AF.Square, accum_out=ssD[:]),
                         reads=[("big", yb)], writes=["cb", "ssD"])
                    S.op("act", lambda e: e.activation(out=rsD[:], in_=ssD[:], func=AF.Ln, scale=1.0 / 1024, bias=EPS),
                         reads=["ssD"], writes=["rsD"])
                    S.op("act", lambda e: e.activation(out=rsD[:], in_=rsD[:], func=AF.Exp, scale=-0.5), reads=["rsD"], writes=["rsD"])
                    S.op("dve", lambda e, yb=yb: e.scalar_tensor_tensor(out=bufA[:], in0=big[yb][:], scalar=rsD[:, 0:1], in1=gpf[:],
                                                                        op0=ALU.mult, op1=ALU.mult),
                         reads=[("big", yb), "rsD", "gpf"], writes=["bufA"])
                    S.op("pool", lambda e, hs=hs: e.tensor_tensor(out=h2t[:], in0=bufA[:], in1=h1t[hs][:], op=ALU.add),
                         reads=["bufA", ("h1t", hs)], writes=["h2t"])
                    S.op("dve", lambda e: e.tensor_copy(out=cb[:], in_=h2t[:]), reads=["h2t"], writes=["cb"])
                    for k in range(8):
                        S.op("pe", lambda e, k=k: e.transpose(out=tpD[:, k, :], in_=cb[:, k * 128:(k + 1) * 128], identity=ident[:]),
                             reads=["cb", "ident"], writes=["tpD"])
                    S.op("act", lambda e: e.copy(out=h2T[:], in_=tpD[:]), reads=["tpD"], writes=["h2T"])
                    gb = nbig % 2
                    nbig += 1
                    for half in range(2):
                        for k in range(8):
                            S.op("pe", lambda e, gb=gb, half=half, k=k: e.matmul(
                                big[gb][:, half * 512:(half + 1) * 512], lhsT=h2T[:, k, :],
                                rhs=wpg[:, k, half * 512:(half + 1) * 512], start=(k == 0), stop=(k == 7)),
                                reads=["h2T"] + WPG, writes=[("big", gb)])
                    S.op("dve", lambda e, gb=gb: e.tensor_tensor(out=bufB[:], in0=big[gb][:], in1=bpe[:], op=ALU.add),
                         reads=[("big", gb), "bpe"], writes=["bufB"])
                    S.op("act", lambda e: e.activation(out=bufB[:], in_=bufB[:], func=AF.Sigmoid), reads=["bufB"], writes=["bufB"])
                    pbk = nbig % 2
                    nbig += 1
                    for half in range(2):
                        for k in range(2):
                            S.op("pe", lambda e, pbk=pbk, half=half, k=k, t=t: e.matmul(
                                big[pbk][:, half * 512:(half + 1) * 512], lhsT=pT[:, k, t * 128:(t + 1) * 128],
                                rhs=wpp[:, k, half * 512:(half + 1) * 512], start=(k == 0), stop=(k == 1)),
                                reads=[("pT", t)] + WPP, writes=[("big", pbk)])
                    S.op("dve", lambda e, pbk=pbk: e.tensor_tensor(out=bufA[:], in0=big[pbk][:], in1=bufB[:], op=ALU.mult),
                         reads=[("big", pbk), "bufB"], writes=["bufA"])
                    S.op("pool", lambda e: e.tensor_tensor(out=bufB[:], in0=bufA[:], in1=h2t[:], op=ALU.add),
                         reads=["bufA", "h2t"], writes=["bufB"])
                    S.op("gq", lambda e, tok=tok: e.dma_start(out=out[tok:tok + 128, :], in_=bufB[:]),
                         reads=["bufB"], writes=["out"])
            S.op("spw", None, reads=["out"])
            S.flush()

    return nc


def _prep_inputs(inputs):
    f = np.float32
    x = np.asarray(inputs["x"], dtype=f)
    p = np.asarray(inputs["p"], dtype=f)
    shared = {
        "w_in": np.ascontiguousarray(inputs["w_in"][0], dtype=f),
        "w_out": np.ascontiguousarray(inputs["w_out"][0], dtype=f),
        "w_gu": np.ascontiguousarray(inputs["w_gate_up"][0], dtype=f),
        "w_dn": np.ascontiguousarray(inputs["w_down"][0], dtype=f),
        "w_pg": np.ascontiguousarray(inputs["w_pe_gate"][0], dtype=f),
        "w_pp": np.ascontiguousarray(inputs["w_pe_proj"][0], dtype=f),
        "wspT": np.ascontiguousarray(np.transpose(np.asarray(inputs["w_spatial"][0], dtype=f), (2, 0, 1))),
        "bsT": np.ascontiguousarray(np.asarray(inputs["b_spatial"][0], dtype=f).T),
        "g_pre": np.ascontiguousarray(np.asarray(inputs["ln_pre_mix"][0], dtype=f).reshape(8, 128).T),
        "g_out": np.ascontiguousarray(np.concatenate([np.asarray(inputs["attn_out_norm"][0], dtype=f),
                                                      np.asarray(inputs["sgu_out_norm"][0], dtype=f)]).reshape(8, 128).T),
        "g_ffn": np.ascontiguousarray(np.asarray(inputs["ln_pre_ffn"][0], dtype=f).reshape(8, 128).T),
        "lng_bc": np.ascontiguousarray(np.broadcast_to(np.tile(np.asarray(inputs["sgu_ln_g"][0], dtype=f), 4)[None, :], (128, 512))),
        "lnb_bc": np.ascontiguousarray(np.broadcast_to(np.tile(np.asarray(inputs["sgu_ln_b"][0], dtype=f), 4)[None, :], (128, 512))),
        "gpm_bc": np.ascontiguousarray(np.broadcast_to(np.asarray(inputs["ln_post_mix"][0], dtype=f)[None, :], (128, 1024))),
        "gpf_bc": np.ascontiguousarray(np.broadcast_to(np.asarray(inputs["ln_post_ffn"][0], dtype=f)[None, :], (128, 1024))),
        "bpe_bc": np.ascontiguousarray(np.broadcast_to(np.asarray(inputs["b_pe_gate"][0], dtype=f)[None, :], (128, 1024))),
    }
    in_maps = []
    for c in range(8):
        b, half = c // 2, c % 2
        t0 = half * NTOK
        if half == 0:
            halo = np.zeros((NHALO, D_MODEL), dtype=f)
        else:
            halo = x[b, t0 - NHALO:t0]
        m = dict(shared)
        m["xc"] = np.ascontiguousarray(np.concatenate([halo, x[b, t0:t0 + NTOK]], axis=0))
        m["pc"] = np.ascontiguousarray(p[0, b, t0:t0 + NTOK])
        m["flag"] = np.full((128, 64), float(half), dtype=f)
        in_maps.append(m)
    return in_maps


_NC_CACHE = {}


def kernel(**inputs):
    in_maps = _prep_inputs(inputs)
    if "nc" not in _NC_CACHE:
        _NC_CACHE["nc"] = build_nc()
    nc = _NC_CACHE["nc"]
    res = run_bass_kernel_spmd(nc, in_maps, core_ids=list(range(8)))
    outp = np.empty((4, 2 * NTOK, D_MODEL), dtype=np.float32)
    for c in range(8):
        b, half = c // 2, c % 2
        outp[b, half * NTOK:(half + 1) * NTOK] = res.results[c]["out"]
    return outp
```

```python
import numpy as np
from contextlib import ExitStack
import concourse.bass as bass
import concourse.mybir as mybir
from concourse.bass_utils import run_bass_kernel_spmd

F32 = mybir.dt.float32
BF16 = mybir.dt.bfloat16
AF = mybir.ActivationFunctionType
ALU = mybir.AluOpType

NTOK = 4096
NHALO = 2048
NKV = NTOK + NHALO
D_MODEL = 1024
D_FF = 2816
NFF = D_FF // 128
EPS = 1e-6

COMPUTE = ("pe", "act", "dve", "pool")
DMAQ = ("sp", "gq")
NSEM_DMA = 8
class Sched:
    def __init__(self, nc, stack):
        self.nc = nc
        self.ops = []
        self.lastw = {}
        self.readers = {}
        self.emitted = 0
        self.sem = {}
        for e in COMPUTE:
            self.sem[e] = stack.enter_context(nc.semaphore("s_" + e))
        self.dsem = {q: [stack.enter_context(nc.semaphore(f"d_{q}{i}")) for i in range(NSEM_DMA)]
                     for q in DMAQ}
        self.sigcount = {e: 0 for e in COMPUTE}
        self.dcount = {q: 0 for q in DMAQ}
        self.waited = {}
        self.last_op = {}
        self._pending_barrier = {}
        self._dma_ids = {}

    @staticmethod
    def stream(eng):
        return "pool" if eng == "gq" else ("sp" if eng == "spw" else eng)

    def op(self, eng, fn, reads=(), writes=()):
        deps = set(self.barrier_deps_for(eng))
        for r in reads:
            if r in self.lastw:
                deps.add(self.lastw[r])
        for w in writes:
            if w in self.lastw:
                deps.add(self.lastw[w])
            for rd in self.readers.get(w, ()):
                deps.add(rd)
        if eng == "spw":
            for q in DMAQ:
                deps.update(self._dma_ids.get(q, [])[-NSEM_DMA:])
        oid = len(self.ops)
        deps.discard(oid)
        self.ops.append(dict(eng=eng, fn=fn, deps=deps, sig=None))
        for r in reads:
            self.readers.setdefault(r, []).append(oid)
        for w in writes:
            self.lastw[w] = oid
            self.readers[w] = []
        if fn is not None:
            self.last_op[self.stream(eng)] = oid
        if eng in DMAQ:
            self._dma_ids.setdefault(eng, []).append(oid)
        return oid

    def barrier_deps_for(self, eng):
        st = self.stream(eng)
        if st in self._pending_barrier:
            d = self._pending_barrier.pop(st)
            return d
        return ()

    def barrier(self):
        deps = set()
        for st, oid in self.last_op.items():
            deps.add(oid)
        for q in DMAQ:
            ids = self._dma_ids.setdefault(q, [])
            deps.update(ids[-NSEM_DMA:])
        self._pending_barrier = {st: set(deps) for st in ("pe", "act", "dve", "pool", "sp")}

    def flush(self, final=False):
        nc = self.nc
        ops = self.ops
        lo = self.emitted
        hi = len(ops)
        needed = set()
        for i in range(lo, hi):
            o = ops[i]
            for d in o["deps"]:
                if d < lo:
                    continue
                de = ops[d]["eng"]
                if de in COMPUTE:
                    if de == "pe" and o["eng"] == "pe":
                        continue
                    needed.add(d)
        for d in needed:
            assert d >= lo or ops[d]["sig"] is not None, "cross-flush dep on unsignalled op"
        last_in = {}
        for i in range(lo, hi):
            if ops[i]["fn"] is not None:
                last_in[ops[i]["eng"]] = i
        for e, i in last_in.items():
            if e in COMPUTE:
                needed.add(i)
        for i in range(lo, hi):
            o = ops[i]
            e = o["eng"]
            if e in COMPUTE:
                if i in needed:
                    assert o["fn"] is not None
                    self.sigcount[e] += 1
                    o["sig"] = (self.sem[e], self.sigcount[e], ("c", e))
            elif e == "spw":
                pass
            else:
                k = self.dcount[e]
                self.dcount[e] += 1
                o["sig"] = (self.dsem[e][k % NSEM_DMA], 16 * (k // NSEM_DMA + 1), ("d", e, k % NSEM_DMA))
                o["dk"] = k
        per_stream = {st: [] for st in ("pe", "act", "dve", "pool", "sp")}
        for i in range(lo, hi):
            per_stream[self.stream(ops[i]["eng"])].append(i)

        def emit(st, engobj):
            for i in per_stream[st]:
                o = ops[i]
                waits = []
                for d in sorted(o["deps"]):
                    od = ops[d]
                    if od["eng"] == "pe" and o["eng"] == "pe":
                        continue
                    if od["sig"] is None:
                        assert d < lo, (d, lo, od["eng"], o["eng"])
                        continue
                    sem, val, key = od["sig"]
                    waits.append((sem, val, key))
                if o["eng"] in DMAQ:
                    k = o["dk"]
                    if k >= NSEM_DMA:
                        q = o["eng"]
                        waits.append((self.dsem[q][k % NSEM_DMA], 16 * (k // NSEM_DMA), ("d", q, k % NSEM_DMA)))
                for sem, val, key in waits:
                    wk = (st, key)
                    if self.waited.get(wk, 0) >= val:
                        continue
                    self.waited[wk] = val
                    engobj.wait_ge(sem, val)
                if o["fn"] is None:
                    continue
                ins = o["fn"](engobj)
                if o["sig"] is not None and o["eng"] in COMPUTE:
                    ins.then_inc(o["sig"][0], 1)
                elif o["eng"] in DMAQ:
                    ins.then_inc(o["sig"][0], 16)

        indeg = {}
        succ = {}
        def _edge(a, b):
            succ.setdefault(a, []).append(b)
            indeg[b] = indeg.get(b, 0) + 1
        for st_, lst in per_stream.items():
            for a, b in zip(lst, lst[1:]):
                _edge(a, b)
        for i in range(lo, hi):
            indeg.setdefault(i, 0)
            for d in ops[i]["deps"]:
                if d >= lo:
                    _edge(d, i)
        ready = [i for i in range(lo, hi) if indeg[i] == 0]
        seen = 0
        while ready:
            a = ready.pop()
            seen += 1
            for b in succ.get(a, ()):
                indeg[b] -= 1
                if indeg[b] == 0:
                    ready.append(b)
        assert seen == hi - lo, f"schedule deadlock: {hi - lo - seen} ops unreachable"

        with nc.Block() as block:
            @block.sync
            def _(e):
                emit("sp", e)

            @block.scalar
            def _(e):
                emit("act", e)

            @block.vector
            def _(e):
                emit("dve", e)

            @block.gpsimd
            def _(e):
                emit("pool", e)

            @block.tensor
            def _(e):
                emit("pe", e)
        self.emitted = hi
        self.barrier()


def build_nc(debug=False, phases="ABCD"):
    nc = bass.Bass("TRN2", target_bir_lowering=False)

    def din(name, shape, dt=F32):
        return nc.dram_tensor(name, list(shape), dt, kind="ExternalInput").ap()

    xc = din("xc", [NKV, D_MODEL])
    pc = din("pc", [NTOK, 256])
    flag = din("flag", [128, 64])
    w_in = din("w_in", [1024, 2560])
    w_out = din("w_out", [1024, 1024])
    w_gu = din("w_gu", [1024, 2 * D_FF])
    w_dn = din("w_dn", [D_FF, 1024])
    w_pg = din("w_pg", [1024, 1024])
    w_pp = din("w_pp", [256, 1024])
    wspT_in = din("wspT", [128, 4, 128])
    bsT_in = din("bsT", [128, 4])
    g_pre_in = din("g_pre", [128, 8])
    g_out_in = din("g_out", [128, 8])
    g_ffn_in = din("g_ffn", [128, 8])
    lng_in = din("lng_bc", [128, 512])
    lnb_in = din("lnb_bc", [128, 512])
    gpm_in = din("gpm_bc", [128, 1024])
    gpf_in = din("gpf_bc", [128, 1024])
    bpe_in = din("bpe_bc", [128, 1024])

    skind = "ExternalOutput" if debug else "Internal"
    out = nc.dram_tensor("out", [NTOK, D_MODEL], F32, kind="ExternalOutput").ap()
    qT_d = nc.dram_tensor("qT_d", [4, 128, NTOK], BF16, kind=skind).ap()
    kT_d = nc.dram_tensor("kT_d", [4, 128, NKV], BF16, kind=skind).ap()
    v_d = nc.dram_tensor("v_d", [NKV, 512], BF16, kind=skind).ap()
    h1_d = nc.dram_tensor("h1_d", [NTOK, D_MODEL], F32, kind=skind).ap()
    gs_d = nc.dram_tensor("gs_d", [128, 4, NTOK], BF16, kind=skind).ap() if debug else None

    with ExitStack() as top:
        S = Sched(nc, top)

        def alloc(st, name, shape, dt):
            return st.enter_context(nc.sbuf_tensor(name + "_sb", list(shape), dt))

        def palloc(st, name, shape, dt):
            return st.enter_context(nc.psum_tensor(name + "_ps", list(shape), dt))

        ident = alloc(top, "ident", [128, 128], BF16)
        ones_bf = alloc(top, "ones_bf", [128, 128], BF16)
        flag_bf = alloc(top, "flag_bf", [128, 64], BF16)
        gstack = ExitStack()
        gsgu = alloc(gstack, "gsgu", [128, 4, NTOK], BF16)
        with ExitStack() as st:
            tmpf = alloc(st, "c_tmpf", [128, 128], F32)
            flg = alloc(st, "c_flg", [128, 64], F32)
            S.op("pool", lambda e: e.memset(tmpf[:], 1.0), writes=["tmpf"])
            S.op("dve", lambda e: e.tensor_copy(out=ones_bf[:], in_=tmpf[:]), reads=["tmpf"], writes=["ones_bf"])
            S.op("pool", lambda e: e.affine_select(out=tmpf[:], in_=tmpf[:], pattern=[[1, 128]],
                                                   compare_op=ALU.is_equal, fill=0.0, base=0,
                                                   channel_multiplier=-1), reads=["tmpf"], writes=["tmpf"])
            S.op("dve", lambda e: e.tensor_copy(out=ident[:], in_=tmpf[:]), reads=["tmpf"], writes=["ident"])
            S.op("sp", lambda e: e.dma_start(out=flg[:], in_=flag[:, :]), writes=["flg"])
            S.op("dve", lambda e: e.tensor_copy(out=flag_bf[:], in_=flg[:]), reads=["flg"], writes=["flag_bf"])
            S.flush()

        if "A" in phases:
          with ExitStack() as st:
            w_in_bf = alloc(st, "w_in_bf", [128, 8, 2560], BF16)
            wstage = [alloc(st, f"wstage{i}", [128, 2560], F32) for i in range(2)]
            g_pre = alloc(st, "g_pre_sb", [128, 8], F32)
            wsp_f = alloc(st, "wsp_f", [128, 4, 128], F32)
            wspT = alloc(st, "wspT", [128, 4, 128], BF16)
            bsT = alloc(st, "bsT", [128, 4], F32)
            lng = alloc(st, "lng", [128, 512], F32)
            lnb = alloc(st, "lnb", [128, 512], F32)
            xs = [alloc(st, f"xs{i}", [128, 4, 1024], F32) for i in range(2)]
            junk = alloc(st, "junkA", [128, 1024], BF16)
            ssq = [alloc(st, f"ssqA{i}", [128, 4], F32) for i in range(2)]
            rsq = [alloc(st, f"rsqA{i}", [128, 4], F32) for i in range(2)]
            ab = [alloc(st, f"abA{i}", [128, 1024], BF16) for i in range(2)]
            aT = [alloc(st, f"aTA{i}", [128, 8, 512], BF16) for i in range(2)]
            kqst = [alloc(st, f"kqst{i}", [128, 512], BF16) for i in range(4)]
            vst = [alloc(st, f"vst{i}", [128, 512], BF16) for i in range(2)]
            guz = alloc(st, "guz", [128, 1024], F32)
            st6 = alloc(st, "st6", [128, 4, 6], F32)
            mv = alloc(st, "mv", [128, 4, 2], F32)
            rstd4 = alloc(st, "rstd4", [128, 4], F32)
            nb4 = alloc(st, "nb4", [128, 4], F32)
            zn = alloc(st, "zn", [128, 512], F32)
            znb = alloc(st, "znb", [128, 512], BF16)
            sg = alloc(st, "sg", [128, 512], F32)
            ss2 = alloc(st, "ss2", [128, 1], F32)
            rs2 = alloc(st, "rs2", [128, 1], F32)
            sgn = alloc(st, "sgn", [128, 512], BF16)
            tp = [palloc(st, f"tpA{i}", [128, 8, 128], BF16) for i in range(2)]
            kq = [palloc(st, f"kqA{i}", [128, 512], F32) for i in range(2)]
            uz = palloc(st, "uzA", [128, 1024], F32)
            vp = palloc(st, "vpA", [128, 512], F32)
            mx = palloc(st, "mxA", [128, 512], F32)

            S.op("sp", lambda e: e.dma_start(out=g_pre[:], in_=g_pre_in[:, :]), writes=["g_pre"])
            S.op("sp", lambda e: e.dma_start(out=wsp_f[:], in_=wspT_in[:, :, :]), writes=["wsp_f"])
            S.op("sp", lambda e: e.dma_start(out=bsT[:], in_=bsT_in[:, :]), writes=["bsT"])
            S.op("sp", lambda e: e.dma_start(out=lng[:], in_=lng_in[:, :]), writes=["lng"])
            S.op("sp", lambda e: e.dma_start(out=lnb[:], in_=lnb_in[:, :]), writes=["lnb"])
            S.op("pool", lambda e: e.affine_select(out=wsp_f[:], in_=wsp_f[:], pattern=[[0, 4], [1, 128]],
                                                   compare_op=ALU.is_ge, fill=0.0, base=0,
                                                   channel_multiplier=-1), reads=["wsp_f"], writes=["wsp_f"])
            S.op("dve", lambda e: e.tensor_copy(out=wspT[:], in_=wsp_f[:]), reads=["wsp_f"], writes=["wspT"])
            for k in range(8):
                s = k % 2
                S.op("sp", lambda e, k=k, s=s: e.dma_start(out=wstage[s][:], in_=w_in[k * 128:(k + 1) * 128, :]),
                     writes=[("wstage", s)])
                if k % 2 == 0:
                    S.op("act", lambda e, k=k, s=s: e.activation(out=w_in_bf[:, k, :], in_=wstage[s][:], func=AF.Identity,
                                                                 scale=g_pre[:, k:k + 1]),
                         reads=[("wstage", s), "g_pre"], writes=[("w_in_bf", k)])
                else:
                    S.op("dve", lambda e, k=k, s=s: e.tensor_scalar(out=w_in_bf[:, k, :], in0=wstage[s][:],
                                                                    scalar1=g_pre[:, k:k + 1], scalar2=None, op0=ALU.mult),
                         reads=[("wstage", s), "g_pre"], writes=[("w_in_bf", k)])
            WIN = [("w_in_bf", k) for k in range(8)]

            nkq = 0
            nv = 0
            for bi in range(12):
                s = bi % 2
                main = bi >= 4
                S.op("sp", lambda e, bi=bi, s=s: e.dma_start(
                    out=xs[s][:], in_=xc[bi * 512:(bi + 1) * 512, :].rearrange("(t p) d -> p t d", p=128)),
                    writes=[("xs", s)])
                for t in range(4):
                    S.op("act", lambda e, s=s, t=t: e.activation(out=junk[:], in_=xs[s][:, t, :], func=AF.Square,
                                                                 accum_out=ssq[s][:, t:t + 1]),
                         reads=[("xs", s)], writes=["junkA", ("ssq", s, t)])
                S.op("act", lambda e, s=s: e.activation(out=rsq[s][:], in_=ssq[s][:], func=AF.Ln, scale=1.0 / 1024, bias=EPS),
                     reads=[("ssq", s, t) for t in range(4)], writes=[("rsq", s)])
                S.op("act", lambda e, s=s: e.activation(out=rsq[s][:], in_=rsq[s][:], func=AF.Exp, scale=-0.5),
                     reads=[("rsq", s)], writes=[("rsq", s)])
                for t in range(4):
                    a = t % 2
                    S.op("dve", lambda e, s=s, t=t, a=a: e.tensor_scalar(out=ab[a][:], in0=xs[s][:, t, :],
                                                                         scalar1=rsq[s][:, t:t + 1], scalar2=None,
                                                                         op0=ALU.mult),
                         reads=[("xs", s), ("rsq", s)], writes=[("ab", a)])
                    for k in range(8):
                        S.op("pe", lambda e, a=a, k=k: e.transpose(out=tp[a][:, k, :], in_=ab[a][:, k * 128:(k + 1) * 128],
                                                                   identity=ident[:]),
                             reads=[("ab", a), "ident"], writes=[("tp", a)])
                    if t % 2 == 0:
                        S.op("act", lambda e, s=s, t=t, a=a: e.copy(out=aT[s][:, :, t * 128:(t + 1) * 128], in_=tp[a][:]),
                             reads=[("tp", a)], writes=[("aT", s, t)])
                    else:
                        S.op("dve", lambda e, s=s, t=t, a=a: e.tensor_copy(out=aT[s][:, :, t * 128:(t + 1) * 128], in_=tp[a][:]),
                             reads=[("tp", a)], writes=[("aT", s, t)])
                AT = [("aT", s, t) for t in range(4)]
                for which in (("k", "q") if main else ("k",)):
                    for c in range(4):
                        pz = nkq % 2
                        ks_ = nkq % 4
                        nkq += 1
                        col0 = (512 if which == "k" else 0) + c * 128
                        for k in range(8):
                            S.op("pe", lambda e, pz=pz, k=k, col0=col0, s=s: e.matmul(
                                kq[pz][:], lhsT=w_in_bf[:, k, col0:col0 + 128], rhs=aT[s][:, k, :],
                                start=(k == 0), stop=(k == 7)),
                                reads=AT + WIN, writes=[("kq", pz)])
                        if which == "k":
                            S.op("act", lambda e, pz=pz, ks_=ks_: e.copy(out=kqst[ks_][:], in_=kq[pz][:]),
                                 reads=[("kq", pz)], writes=[("kqst", ks_)])
                            S.op("gq", lambda e, ks_=ks_, c=c, bi=bi: e.dma_start(
                                out=kT_d[c, :, bi * 512:(bi + 1) * 512], in_=kqst[ks_][:]),
                                reads=[("kqst", ks_)], writes=["kT_d"])
                        else:
                            S.op("dve", lambda e, pz=pz, ks_=ks_: e.tensor_scalar(out=kqst[ks_][:], in0=kq[pz][:],
                                                                                  scalar1=0.125, scalar2=None,
                                                                                  op0=ALU.mult),
                                 reads=[("kq", pz)], writes=[("kqst", ks_)])
                            S.op("gq", lambda e, ks_=ks_, c=c, bi=bi: e.dma_start(
                                out=qT_d[c, :, (bi - 4) * 512:(bi - 3) * 512], in_=kqst[ks_][:]),
                                reads=[("kqst", ks_)], writes=["qT_d"])
                for t in range(4):
                    vs_ = nv % 2
                    nv += 1
                    for k in range(8):
                        S.op("pe", lambda e, k=k, s=s, t=t: e.matmul(
                            vp[:], lhsT=aT[s][:, k, t * 128:(t + 1) * 128], rhs=w_in_bf[:, k, 1024:1536],
                            start=(k == 0), stop=(k == 7)),
                            reads=AT + WIN, writes=["vp"])
                    S.op("act", lambda e, vs_=vs_: e.copy(out=vst[vs_][:], in_=vp[:]), reads=["vp"], writes=[("vst", vs_)])
                    S.op("gq", lambda e, vs_=vs_, bi=bi, t=t: e.dma_start(
                        out=v_d[bi * 512 + t * 128: bi * 512 + (t + 1) * 128, :], in_=vst[vs_][:]),
                        reads=[("vst", vs_)], writes=["v_d"])
                    if not main:
                        continue
                    tok = (bi - 4) * 512 + t * 128
                    for hf in range(2):
                        for k in range(8):
                            S.op("pe", lambda e, k=k, s=s, t=t, hf=hf: e.matmul(
                                uz[:, hf * 512:(hf + 1) * 512], lhsT=aT[s][:, k, t * 128:(t + 1) * 128],
                                rhs=w_in_bf[:, k, 1536 + hf * 512:2048 + hf * 512], start=(k == 0), stop=(k == 7)),
                                reads=AT + WIN, writes=["uz"])
                    S.op("act", lambda e: e.activation(out=guz[:], in_=uz[:], func=AF.Gelu_apprx_tanh),
                         reads=["uz"], writes=["guz"])
                    for g in range(4):
                        S.op("dve", lambda e, g=g: e.bn_stats(out=st6[:, g, :], in_=guz[:, 512 + g * 128:512 + (g + 1) * 128]),
                             reads=["guz"], writes=[("st6", g)])
                    for g in range(4):
                        S.op("dve", lambda e, g=g: e.bn_aggr(out=mv[:, g, :], in_=st6[:, g, :]),
                             reads=[("st6", g)], writes=[("mv", g)])
                    MV = [("mv", g) for g in range(4)]
                    S.op("act", lambda e: e.activation(out=rstd4[:], in_=mv[:, :, 1], func=AF.Ln, bias=EPS),
                         reads=MV, writes=["rstd4"])
                    S.op("act", lambda e: e.activation(out=rstd4[:], in_=rstd4[:], func=AF.Exp, scale=-0.5),
                         reads=["rstd4"], writes=["rstd4"])
                    S.op("dve", lambda e: e.scalar_tensor_tensor(out=nb4[:], in0=mv[:, :, 0], scalar=-1.0, in1=rstd4[:],
                                                                 op0=ALU.mult, op1=ALU.mult),
                         reads=MV + ["rstd4"], writes=["nb4"])
                    for g in range(4):
                        S.op("dve", lambda e, g=g: e.tensor_scalar(out=zn[:, g * 128:(g + 1) * 128],
                                                                   in0=guz[:, 512 + g * 128:512 + (g + 1) * 128],
                                                                   scalar1=rstd4[:, g:g + 1], scalar2=nb4[:, g:g + 1],
                                                                   op0=ALU.mult, op1=ALU.add),
                             reads=["guz", "rstd4", "nb4"], writes=[("zn", g)])
                    ZN = [("zn", g) for g in range(4)]
                    S.op("dve", lambda e: e.tensor_tensor(out=zn[:], in0=zn[:], in1=lng[:], op=ALU.mult),
                         reads=ZN + ["lng"], writes=ZN)
                    S.op("dve", lambda e: e.tensor_tensor(out=znb[:], in0=zn[:], in1=lnb[:], op=ALU.add),
                         reads=ZN + ["lnb"], writes=["znb"])
                    for g in range(4):
                        S.op("pe", lambda e, g=g: e.matmul(mx[:, g * 128:(g + 1) * 128], lhsT=wspT[:, g, :],
                                                           rhs=znb[:, g * 128:(g + 1) * 128], start=True, stop=True),
                             reads=["znb", "wspT"], writes=["mx"])
                    for g in range(4):
                        S.op("dve", lambda e, g=g: e.scalar_tensor_tensor(
                            out=sg[:, g * 128:(g + 1) * 128], in0=mx[:, g * 128:(g + 1) * 128], scalar=bsT[:, g:g + 1],
                            in1=guz[:, g * 128:(g + 1) * 128], op0=ALU.add, op1=ALU.mult),
                            reads=["mx", "guz", "bsT"], writes=[("sg", g)])
                    SG = [("sg", g) for g in range(4)]
                    S.op("act", lambda e: e.activation(out=junk[:, 0:512], in_=sg[:], func=AF.Square, accum_out=ss2[:]),
                         reads=SG, writes=["junkA", "ss2"])
                    S.op("act", lambda e: e.activation(out=rs2[:], in_=ss2[:], func=AF.Ln, scale=1.0 / 512, bias=EPS),
                         reads=["ss2"], writes=["rs2"])
                    S.op("act", lambda e: e.activation(out=rs2[:], in_=rs2[:], func=AF.Exp, scale=-0.5),
                         reads=["rs2"], writes=["rs2"])
                    S.op("dve", lambda e: e.tensor_scalar(out=sgn[:], in0=sg[:], scalar1=rs2[:, 0:1], scalar2=None,
                                                          op0=ALU.mult),
                         reads=SG + ["rs2"], writes=["sgn"])
                    a2 = t % 2
                    for g in range(4):
                        S.op("pe", lambda e, g=g, a2=a2: e.transpose(out=tp[a2][:, g, :], in_=sgn[:, g * 128:(g + 1) * 128],
                                                                     identity=ident[:]),
                             reads=["sgn", "ident"], writes=[("tp", a2)])
                    S.op("dve", lambda e, a2=a2, tok=tok: e.tensor_copy(out=gsgu[:, :, tok:tok + 128], in_=tp[a2][:, 0:4, :]),
                         reads=[("tp", a2)], writes=[("gsgu", tok // 128)])
            if debug:
                S.op("sp", lambda e: e.dma_start(out=gs_d[:, :, :], in_=gsgu[:]),
                     reads=[("gsgu", i) for i in range(32)], writes=["gs_d"])
            S.flush()

        if "B" in phases:
          with ExitStack() as st:
            w_out_bf = alloc(st, "w_out_bf", [128, 8, 1024], BF16)
            g_out = alloc(st, "g_outB", [128, 8], F32)
            gpm = alloc(st, "gpmB", [128, 1024], F32)
            Mk = [[alloc(st, f"Mk{j}_{di}", [128, 2, 2, 128], BF16) for di in range(3)] for j in range(4)]
            iot = alloc(st, "iotB", [128, 2, 128], F32)
            mtmp = alloc(st, "mtmpB", [128, 2, 128], F32)
            qs = [alloc(st, f"qsB{i}", [128, 2048], BF16) for i in range(2)]
            ks = [alloc(st, f"ksB{i}", [128, 4096], BF16) for i in range(2)]
            V1 = [alloc(st, f"V1B{i}", [128, 17, 128], BF16) for i in range(2)]
            V4 = [alloc(st, f"V4B{i}", [128, 5, 4, 128], BF16) for i in range(2)]
            V16 = [alloc(st, f"V16B{i}", [128, 2, 16, 128], BF16) for i in range(2)]
            acc = alloc(st, "accB", [128, 2, 2048], F32)
            wstg = [acc[:, i, 0:1024] for i in range(2)]
            outT = alloc(st, "outTB", [128, 4, 2048], F32)
            Eb = [alloc(st, f"EbB{i}", [128, 512], BF16) for i in range(2)]
            Em = [alloc(st, f"EmB{i}", [128, 2, 2, 128], BF16) for i in range(3)]
            sq = alloc(st, "sqB", [128, 4, 512], BF16)
            rb = alloc(st, "rbB", [128, 512], F32)
            gTa = alloc(st, "gTaB", [128, 4, 512], BF16)
            xt = [alloc(st, f"xtB{i}", [128, 1024], F32) for i in range(2)]
            junkB = alloc(st, "junkB", [128, 1024], BF16)
            ss1 = alloc(st, "ss1B", [128, 1], F32)
            rs1 = alloc(st, "rs1B", [128, 1], F32)
            t1 = alloc(st, "t1B", [128, 1024], F32)
            sc = [palloc(st, f"scB{i}", [128, 2, 512], F32) for i in range(2)]
            pvb = [palloc(st, f"pvB{i}", [128, 512], F32) for i in range(2)]
            pv = [pvb[i][:, 0:256].rearrange("p (a q) -> p a q", a=2) for i in range(2)]
            ssp = palloc(st, "sspB", [128, 512], F32)
            mixed = sc[0][:, :, :].rearrange("p h c -> p (h c)")

            S.op("sp", lambda e: e.dma_start(out=g_out[:], in_=g_out_in[:, :]), writes=["g_out"])
            S.op("sp", lambda e: e.dma_start(out=gpm[:], in_=gpm_in[:, :]), writes=["gpm"])
            for k in range(8):
                s = k % 2
                S.op("sp", lambda e, k=k, s=s: e.dma_start(out=wstg[s], in_=w_out[k * 128:(k + 1) * 128, :]),
                     writes=[("wstg", s)])
                S.op("dve", lambda e, k=k, s=s: e.tensor_scalar(out=w_out_bf[:, k, :], in0=wstg[s],
                                                                scalar1=g_out[:, k:k + 1], scalar2=None, op0=ALU.mult),
                     reads=[("wstg", s), "g_out"], writes=[("w_out_bf", k)])
            WOUT = [("w_out_bf", k) for k in range(8)]
            S.op("pool", lambda e: e.iota(iot[:, 0, :], pattern=[[1, 128]], base=128, channel_multiplier=-1,
                                          allow_small_or_imprecise_dtypes=True), writes=["iot0"])
            S.op("pool", lambda e: e.iota(iot[:, 1, :], pattern=[[1, 128]], base=0, channel_multiplier=-1,
                                          allow_small_or_imprecise_dtypes=True), writes=["iot1"])
            S.op("dve", lambda e: e.tensor_scalar(out=iot[:], in0=iot[:], scalar1=0.0, scalar2=None, op0=ALU.max),
                 reads=["iot0", "iot1"], writes=["iot"])
            for j in range(4):
                for di, D in enumerate((1, 4, 16)):
                    for hl in range(2):
                        h = 2 * j + hl
                        slope = 2.0 ** (-(h + 1))
                        S.op("act", lambda e, slope=slope, D=D: e.activation(out=mtmp[:], in_=iot[:], func=AF.Exp,
                                                                             scale=-slope * D),
                             reads=["iot"], writes=["mtmp"])
                        S.op("pool", lambda e, j=j, di=di, hl=hl: e.affine_select(
                            out=Mk[j][di][:, hl, 0, :], in_=mtmp[:, 0, :], pattern=[[-1, 128]], compare_op=ALU.is_ge,
                            fill=0.0, base=0, channel_multiplier=1), reads=["mtmp"], writes=[("Mk", j, di, hl, 0)])
                        S.op("pool", lambda e, j=j, di=di, hl=hl: e.affine_select(
                            out=Mk[j][di][:, hl, 1, :], in_=mtmp[:, 1, :], pattern=[[1, 128]], compare_op=ALU.is_ge,
                            fill=0.0, base=0, channel_multiplier=-1), reads=["mtmp"], writes=[("Mk", j, di, hl, 1)])

            def accres(D, nd, r):
                if D == 16:
                    return [("acc", t, r) for t in range(16)]
                if D == 4:
                    return [("acc", nd * 4 + t, r + 4 * m) for t in range(4) for m in range(4)]
                return [("acc", nd, m) for m in range(16)]

            ALLACC = [("acc", t, m) for t in range(16) for m in range(16)]
            nblk = 0
            nit = 0
            nx = 0
            import os
            NSPAN = int(os.environ.get("NSPAN", "2")); NJ = int(os.environ.get("NJ", "4")); NDIL = int(os.environ.get("NDIL", "3")); NND = int(os.environ.get("NND", "99")); NOC = int(os.environ.get("NOC", "0"))
            for n in range(NSPAN):
                S0 = NHALO + n * 2048
                for j in range(NJ):
                    sl = nit % 2
                    nit += 1
                    jc = slice(j * 128, (j + 1) * 128)
                    S.op("sp", lambda e, sl=sl, j=j, n=n: e.dma_start(out=qs[sl][:], in_=qT_d[j, :, n * 2048:(n + 1) * 2048]),
                         reads=["qT_d"], writes=[("qs", sl)])
                    S.op("sp", lambda e, sl=sl, j=j, S0=S0: e.dma_start(out=ks[sl][:], in_=kT_d[j, :, S0 - 2048:S0 + 2048]),
                         reads=["kT_d"], writes=[("ks", sl)])
                    for t0 in range(0, 17, 5):
                        t1_ = min(17, t0 + 5)
                        S.op("sp", lambda e, sl=sl, jc=jc, S0=S0, t0=t0, t1_=t1_: e.dma_start(
                            out=V1[sl][:, t0:t1_, :],
                            in_=v_d[S0 - 128 + t0 * 128:S0 - 128 + t1_ * 128, jc].rearrange("(t p) c -> p t c", p=128)),
                            reads=["v_d"], writes=[("V1", sl, t0)])
                    for n4 in range(5):
                        S.op("sp", lambda e, sl=sl, jc=jc, S0=S0, n4=n4: e.dma_start(
                            out=V4[sl][:, n4, :, :],
                            in_=v_d[S0 - 512 + n4 * 512:S0 + n4 * 512, jc].rearrange("(p r) c -> p r c", r=4)),
                            reads=["v_d"], writes=[("V4", sl, n4)])
                    for n16 in range(2):
                        for r0 in range(0, 16, 4):
                            S.op("sp", lambda e, sl=sl, jc=jc, S0=S0, n16=n16, r0=r0: e.dma_start(
                                out=V16[sl][:, n16, r0:r0 + 4, :],
                                in_=v_d[S0 - 2048 + n16 * 2048:S0 + n16 * 2048, jc].rearrange(
                                    "(p r) c -> p r c", r=16)[:, r0:r0 + 4, :]),
                                reads=["v_d"], writes=[("V16", sl, n16, r0)])
                    LOADS = ([("qs", sl), ("ks", sl)] + [("V1", sl, t0) for t0 in range(0, 17, 5)]
                             + [("V4", sl, n4) for n4 in range(5)]
                             + [("V16", sl, a, b) for a in range(2) for b in range(0, 16, 4)])
                    units = []
                    for di, D in ((2, 16), (1, 4), (0, 1))[:NDIL]:
                        for nd in range(min(16 // D, NND)):
                            for r in range(D):
                                base = nd * 128 * D + r
                                qsl = slice(base, base + 127 * D + 1, D)
                                kp = slice(2048 + base - 128 * D, 2048 + base - 128 * D + 127 * D + 1, D)
                                kc = slice(2048 + base, 2048 + base + 127 * D + 1, D)
                                if D == 16:
                                    Vp, Vc = V16[sl][:, 0, r, :], V16[sl][:, 1, r, :]
                                elif D == 4:
                                    Vp, Vc = V4[sl][:, nd, r, :], V4[sl][:, nd + 1, r, :]
                                else:
                                    Vp, Vc = V1[sl][:, nd, :], V1[sl][:, nd + 1, :]
                                halo_prev = (n == 0 and nd == 0)
                                b2 = nblk % 2
                                b3 = nblk % 3
                                nblk += 1
                                def front(b2=b2, b3=b3, kp=kp, kc=kc, qsl=qsl, sl=sl, j=j, di=di, LOADS=LOADS):
                                  for hl in range(2):
                                    hp = slice(hl * 64, (hl + 1) * 64)
                                    for kb, ksl in ((0, kp), (1, kc)):
                                        S.op("pe", lambda e, b2=b2, hl=hl, kb=kb, hp=hp, ksl=ksl, qsl=qsl, sl=sl: e.matmul(
                                            sc[b2][:, hl, kb * 128:(kb + 1) * 128], lhsT=ks[sl][hp, ksl], rhs=qs[sl][hp, qsl],
                                            start=True, stop=True),
                                            reads=LOADS, writes=[("sc", b2)])
                                  S.op("act", lambda e, b2=b2: e.activation(out=Eb[b2][:].rearrange("p (h c) -> p h c", h=2), in_=sc[b2][:, :, 0:256], func=AF.Exp),
                                     reads=[("sc", b2)], writes=[("Eb", b2)])
                                  S.op("dve", lambda e, b2=b2, b3=b3, j=j, di=di: e.tensor_tensor(
                                    out=Em[b3][:], in0=Eb[b2][:].rearrange("p (h k q) -> p h k q", h=2, k=2), in1=Mk[j][di][:], op=ALU.mult),
                                    reads=[("Eb", b2)] + [("Mk", j, di, a, b) for a in range(2) for b in range(2)],
                                    writes=[("Em", b3)])

                                def back(b2=b2, b3=b3, Vp=Vp, Vc=Vc, halo_prev=halo_prev, D=D, nd=nd, r=r, qsl=qsl, LOADS=LOADS):
                                  for hl in range(2):
                                    hp = slice(hl * 64, (hl + 1) * 64)
                                    onesp = flag_bf[:, :] if halo_prev else ones_bf[:, 0:64]
                                    S.op("pe", lambda e, b2=b2, b3=b3, hl=hl, hp=hp, Vp=Vp: e.matmul(
                                        pv[b2][hp, 0, :], lhsT=Vp[:, hp], rhs=Em[b3][:, hl, 0, :], start=True, stop=False),
                                        reads=LOADS + [("Em", b3)], writes=[("pv", b2)])
                                    S.op("pe", lambda e, b2=b2, b3=b3, hl=hl, hp=hp, Vc=Vc: e.matmul(
                                        pv[b2][hp, 0, :], lhsT=Vc[:, hp], rhs=Em[b3][:, hl, 1, :], start=False, stop=True),
                                        reads=LOADS + [("Em", b3)], writes=[("pv", b2)])
                                    S.op("pe", lambda e, b2=b2, b3=b3, hl=hl, hp=hp, onesp=onesp: e.matmul(
                                        pv[b2][hp, 1, :], lhsT=onesp, rhs=Em[b3][:, hl, 0, :], start=True, stop=False),
                                        reads=[("Em", b3), "flag_bf", "ones_bf"], writes=[("pv", b2)])
                                    S.op("pe", lambda e, b2=b2, b3=b3, hl=hl, hp=hp: e.matmul(
                                        pv[b2][hp, 1, :], lhsT=ones_bf[:, 0:64], rhs=Em[b3][:, hl, 1, :], start=False, stop=True),
                                        reads=[("Em", b3), "ones_bf"], writes=[("pv", b2)])
                                  AR = accres(D, nd, r)
                                  if D == 16:
                                    S.op("act", lambda e, b2=b2, qsl=qsl: e.copy(out=acc[:, :, qsl], in_=pv[b2]),
                                         reads=[("pv", b2)], writes=AR)
                                  else:
                                    S.op("dve", lambda e, b2=b2, qsl=qsl: e.tensor_tensor(
                                        out=acc[:, :, qsl], in0=pv[b2], in1=acc[:, :, qsl], op=ALU.add),
                                        reads=[("pv", b2)] + AR, writes=AR)
                                units.append((front, back))
                    for ui, (fr, bk) in enumerate(units):
                        fr()
                        if ui >= 1:
                            units[ui - 1][1]()
                    if units:
                        units[-1][1]()
                    S.op("act", lambda e: e.activation(out=acc[:, 1, :], in_=acc[:, 1, :], func=AF.Ln), reads=ALLACC, writes=ALLACC)
                    S.op("act", lambda e: e.activation(out=acc[:, 1, :], in_=acc[:, 1, :], func=AF.Exp, scale=-1.0),
                         reads=ALLACC, writes=ALLACC)
                    S.op("dve", lambda e, j=j: e.tensor_tensor(out=outT[:, j, :], in0=acc[:, 0, :], in1=acc[:, 1, :], op=ALU.mult),
                         reads=ALLACC, writes=[("outT", j)])
                OUTT = [("outT", j) for j in range(4)]
                for b in range(0 if NOC else 4):
                    cs = slice(b * 512, (b + 1) * 512)
                    for j in range(4):
                        S.op("act", lambda e, j=j, cs=cs: e.activation(out=sq[:, j, :], in_=outT[:, j, cs], func=AF.Square),
                             reads=OUTT, writes=[("sq", j)])
                    for j in range(4):
                        S.op("pe", lambda e, j=j: e.matmul(ssp[:], lhsT=ones_bf[:], rhs=sq[:, j, :], start=(j == 0), stop=(j == 3)),
                             reads=[("sq", j), "ones_bf"], writes=["ssp"])
                    S.op("act", lambda e: e.activation(out=rb[:], in_=ssp[:], func=AF.Ln, scale=1.0 / 512, bias=EPS),
                         reads=["ssp"], writes=["rb"])
                    S.op("act", lambda e: e.activation(out=rb[:], in_=rb[:], func=AF.Exp, scale=-0.5), reads=["rb"], writes=["rb"])
                    for j in range(4):
                        S.op("dve", lambda e, j=j, cs=cs: e.tensor_tensor(out=gTa[:, j, :], in0=outT[:, j, cs], in1=rb[:], op=ALU.mult),
                             reads=OUTT + ["rb"], writes=[("gTa", j)])
                    GTA = [("gTa", j) for j in range(4)]
                    for t in range(4):
                        tok = n * 2048 + b * 512 + t * 128
                        xsl = nx % 2
                        nx += 1
                        S.op("sp", lambda e, xsl=xsl, tok=tok: e.dma_start(out=xt[xsl][:], in_=xc[NHALO + tok:NHALO + tok + 128, :]),
                             writes=[("xt", xsl)])
                        for hf in range(2):
                            for k in range(8):
                                if k < 4:
                                    lh = gTa[:, k, t * 128:(t + 1) * 128]
                                else:
                                    lh = gsgu[:, k - 4, tok:tok + 128]
                                S.op("pe", lambda e, lh=lh, k=k, hf=hf: e.matmul(
                                    mixed[:, hf * 512:(hf + 1) * 512], lhsT=lh, rhs=w_out_bf[:, k, hf * 512:(hf + 1) * 512],
                                    start=(k == 0), stop=(k == 7)),
                                    reads=GTA + WOUT + [("gsgu", tok // 128)], writes=[("sc", 0)])
                        S.op("act", lambda e: e.activation(out=junkB[:], in_=mixed, func=AF.Square, accum_out=ss1[:]),
                             reads=[("sc", 0)], writes=["junkB", "ss1"])
                        S.op("act", lambda e: e.activation(out=rs1[:], in_=ss1[:], func=AF.Ln, scale=1.0 / 1024, bias=EPS),
                             reads=["ss1"], writes=["rs1"])
                        S.op("act", lambda e: e.activation(out=rs1[:], in_=rs1[:], func=AF.Exp, scale=-0.5),
                             reads=["rs1"], writes=["rs1"])
                        S.op("dve", lambda e: e.scalar_tensor_tensor(out=t1[:], in0=mixed, scalar=rs1[:, 0:1], in1=gpm[:],
                                                                     op0=ALU.mult, op1=ALU.mult),
                             reads=[("sc", 0), "rs1", "gpm"], writes=["t1"])
                        S.op("pool", lambda e, xsl=xsl: e.tensor_tensor(out=xt[xsl][:], in0=t1[:], in1=xt[xsl][:], op=ALU.add),
                             reads=["t1", ("xt", xsl)], writes=[("xt", xsl)])
                        S.op("gq", lambda e, xsl=xsl, tok=tok: e.dma_start(out=h1_d[tok:tok + 128, :], in_=xt[xsl][:]),
                             reads=[("xt", xsl)], writes=["h1_d"])
            S.flush()

        gstack.close()
        if "D" in phases:
          with ExitStack() as st:
            wgu = alloc(st, "wgu_bf", [128, 8, 2 * D_FF], BF16)
            wdn = alloc(st, "wdn_bf", [128, NFF, 1024], BF16)
            wpg = alloc(st, "wpg_bf", [128, 8, 1024], BF16)
            wpp = alloc(st, "wpp_bf", [128, 2, 1024], BF16)
            g_ffn = alloc(st, "g_ffnD", [128, 8], F32)
            NB = 256
            with ExitStack() as st2:
                stg = [alloc(st2, f"stgD{i}", [128, 1408], F32) for i in range(3)]
                S.op("sp", lambda e: e.dma_start(out=g_ffn[:], in_=g_ffn_in[:, :]), writes=["g_ffn"])
                pieces = []
                for k in range(8):
                    for q4 in range(4):
                        pieces.append(("gu", k, q4))
                for c in range(NFF):
                    pieces.append(("dn", c, 0))
                for k in range(8):
                    pieces.append(("pg", k, 0))
                for k in range(2):
                    pieces.append(("pp", k, 0))
                for i, (kind, k, q4) in enumerate(pieces):
                    s = i % 3
                    if kind == "gu":
                        src = w_gu[k * 128:(k + 1) * 128, q4 * 1408:(q4 + 1) * 1408]
                        dst = wgu[:, k, q4 * 1408:(q4 + 1) * 1408]
                        w = 1408
                    elif kind == "dn":
                        src = w_dn[k * 128:(k + 1) * 128, :]
                        dst = wdn[:, k, :]
                        w = 1024
                    elif kind == "pg":
                        src = w_pg[k * 128:(k + 1) * 128, :]
                        dst = wpg[:, k, :]
                        w = 1024
                    else:
                        src = w_pp[k * 128:(k + 1) * 128, :]
                        dst = wpp[:, k, :]
                        w = 1024
                    S.op("sp", lambda e, s=s, src=src, w=w: e.dma_start(out=stg[s][:, 0:w], in_=src), writes=[("stg", s)])
                    eng = ("act", "dve", "pool")[i % 3]
                    if kind == "gu":
                        if eng == "act":
                            S.op("act", lambda e, s=s, dst=dst, w=w, k=k: e.activation(out=dst, in_=stg[s][:, 0:w], func=AF.Identity,
                                                                                       scale=g_ffn[:, k:k + 1]),
                                 reads=[("stg", s), "g_ffn"], writes=[("wD", i)])
                        else:
                            S.op(eng, lambda e, s=s, dst=dst, w=w, k=k: e.tensor_scalar(out=dst, in0=stg[s][:, 0:w],
                                                                                        scalar1=g_ffn[:, k:k + 1], scalar2=None,
                                                                                        op0=ALU.mult),
                                 reads=[("stg", s), "g_ffn"], writes=[("wD", i)])
                    else:
                        if eng == "act":
                            S.op("act", lambda e, s=s, dst=dst, w=w: e.copy(out=dst, in_=stg[s][:, 0:w]),
                                 reads=[("stg", s)], writes=[("wD", i)])
                        else:
                            S.op(eng, lambda e, s=s, dst=dst, w=w: e.tensor_copy(out=dst, in_=stg[s][:, 0:w]),
                                 reads=[("stg", s)], writes=[("wD", i)])
                S.flush()
            gpf = alloc(st, "gpfD", [128, 1024], F32)
            bpe = alloc(st, "bpeD", [128, 1024], F32)
            fT = alloc(st, "fTD", [128, 8, NB], BF16)
            pT = alloc(st, "pTD", [128, 2, NB], BF16)
            actT = alloc(st, "actTD", [128, NFF, NB], BF16)
            h2T = alloc(st, "h2TD", [128, 8, 128], BF16)
            h1t = [alloc(st, f"h1tD{i}", [128, 1024], F32) for i in range(2)]
            bufA = alloc(st, "bufAD", [128, 1024], F32)
            bufB = alloc(st, "bufBD", [128, 1024], F32)
            h2t = alloc(st, "h2tD", [128, 1024], F32)
            cb = alloc(st, "cbD", [128, 1024], BF16)
            junkD = alloc(st, "junkD", [128, 1024], BF16)
            pt = [alloc(st, f"ptD{i}", [128, 256], F32) for i in range(2)]
            pb = alloc(st, "pbD", [128, 256], BF16)
            slu = [alloc(st, f"sluD{i}", [128, NB], F32) for i in range(2)]
            ssD = alloc(st, "ssD", [128, 1], F32)
            rsD = alloc(st, "rsD", [128, 1], F32)
            tpD = palloc(st, "tpD", [128, 8, 128], BF16)
            gup = [palloc(st, f"gupD{i}", [128, 2, NB], F32) for i in range(3)]
            big = [palloc(st, f"bigD{i}", [128, 1024], F32) for i in range(2)]

            S.op("sp", lambda e: e.dma_start(out=gpf[:], in_=gpf_in[:, :]), writes=["gpf"])
            S.op("sp", lambda e: e.dma_start(out=bpe[:], in_=bpe_in[:, :]), writes=["bpe"])
            nh = 0
            ng = 0
            nbig = 0
            ncp = 0
            for blk in range(NTOK // NB):
                hslots = []
                for t in range(NB // 128):
                    tok = blk * NB + t * 128
                    hs = nh % 2
                    nh += 1
                    hslots.append(hs)
                    S.op("sp", lambda e, hs=hs, tok=tok: e.dma_start(out=h1t[hs][:], in_=h1_d[tok:tok + 128, :]),
                         reads=["h1_d"], writes=[("h1t", hs)])
                    S.op("sp", lambda e, hs=hs, tok=tok: e.dma_start(out=pt[hs][:], in_=pc[tok:tok + 128, :]),
                         writes=[("pt", hs)])
                    S.op("act", lambda e, hs=hs: e.activation(out=junkD[:], in_=h1t[hs][:], func=AF.Square, accum_out=ssD[:]),
                         reads=[("h1t", hs)], writes=["junkD", "ssD"])
                    S.op("act", lambda e: e.activation(out=rsD[:], in_=ssD[:], func=AF.Ln, scale=1.0 / 1024, bias=EPS),
                         reads=["ssD"], writes=["rsD"])
                    S.op("act", lambda e: e.activation(out=rsD[:], in_=rsD[:], func=AF.Exp, scale=-0.5), reads=["rsD"], writes=["rsD"])
                    S.op("dve", lambda e, hs=hs: e.tensor_scalar(out=cb[:], in0=h1t[hs][:], scalar1=rsD[:, 0:1], scalar2=None,
                                                                 op0=ALU.mult),
                         reads=[("h1t", hs), "rsD"], writes=["cb"])
                    for k in range(8):
                        S.op("pe", lambda e, k=k: e.transpose(out=tpD[:, k, :], in_=cb[:, k * 128:(k + 1) * 128], identity=ident[:]),
                             reads=["cb", "ident"], writes=["tpD"])
                    S.op("act", lambda e, t=t: e.copy(out=fT[:, :, t * 128:(t + 1) * 128], in_=tpD[:]),
                         reads=["tpD"], writes=[("fT", t)])
                    S.op("dve", lambda e, hs=hs: e.tensor_copy(out=pb[:], in_=pt[hs][:]), reads=[("pt", hs)], writes=["pb"])
                    for k in range(2):
                        S.op("pe", lambda e, k=k: e.transpose(out=tpD[:, k, :], in_=pb[:, k * 128:(k + 1) * 128], identity=ident[:]),
                             reads=["pb", "ident"], writes=["tpD"])
                    S.op("dve", lambda e, t=t: e.tensor_copy(out=pT[:, :, t * 128:(t + 1) * 128], in_=tpD[:, 0:2, :]),
                         reads=["tpD"], writes=[("pT", t)])
                FT = [("fT", t) for t in range(NB // 128)]
                WGU = [("wD", i) for i in range(32)]
                WDN = [("wD", 32 + i) for i in range(NFF)]
                WPG = [("wD", 32 + NFF + i) for i in range(8)]
                WPP = [("wD", 40 + NFF + i) for i in range(2)]
                for c in range(NFF):
                    gs_ = ng % 3
                    s2 = ng % 2
                    ng += 1
                    for half in range(2):
                        col0 = half * D_FF + c * 128
                        for k in range(8):
                            S.op("pe", lambda e, gs_=gs_, half=half, col0=col0, k=k: e.matmul(
                                gup[gs_][:, half, :], lhsT=wgu[:, k, col0:col0 + 128], rhs=fT[:, k, :],
                                start=(k == 0), stop=(k == 7)),
                                reads=FT + WGU, writes=[("gup", gs_)])
                    S.op("act", lambda e, gs_=gs_, s2=s2: e.activation(out=slu[s2][:], in_=gup[gs_][:, 0, :], func=AF.Silu),
                         reads=[("gup", gs_)], writes=[("slu", s2)])
                    S.op("dve", lambda e, gs_=gs_, s2=s2, c=c: e.tensor_tensor(out=actT[:, c, :], in0=gup[gs_][:, 1, :],
                                                                               in1=slu[s2][:], op=ALU.mult),
                         reads=[("gup", gs_), ("slu", s2)], writes=[("actT", c)])
                ACTT = [("actT", c) for c in range(NFF)]
                ybs = []
                for t in range(NB // 128):
                    yb = nbig % 2
                    nbig += 1
                    ybs.append(yb)
                    for half in range(2):
                        for c in range(NFF):
                            S.op("pe", lambda e, yb=yb, half=half, c=c, t=t: e.matmul(
                                big[yb][:, half * 512:(half + 1) * 512], lhsT=actT[:, c, t * 128:(t + 1) * 128],
                                rhs=wdn[:, c, half * 512:(half + 1) * 512], start=(c == 0), stop=(c == NFF - 1)),
                                reads=ACTT + WDN, writes=[("big", yb)])
                for t in range(NB // 128):
                    tok = blk * NB + t * 128
                    hs = hslots[t]
                    yb = ybs[t]
                    S.op("act", lambda e, yb=yb: e.activation(out=junkD[:], in_=big[yb][:], func=AF.Square, accum_out=ssD[:]),
                         reads=[("big", yb)], writes=["junkD", "ssD"])
                    S.op("act", lambda e: e.activation(out=rsD[:], in_=ssD[:], func=AF.Ln, scale=1.0 / 1024, bias=EPS),
                         reads=["ssD"], writes=["rsD"])
                    S.op("act", lambda e: e.activation(out=rsD[:], in_=rsD[:], func=AF.Exp, scale=-0.5), reads=["rsD"], writes=["rsD"])
                    S.op("dve", lambda e, yb=yb: e.scalar_tensor_tensor(out=bufA[:], in0=big[yb][:], scalar=rsD[:, 0:1], in1=gpf[:],
                                                                        op0=ALU.mult, op1=ALU.mult),
                         reads=[("big", yb), "rsD", "gpf"], writes=["bufA"])
                    S.op("pool", lambda e, hs=hs: e.tensor_tensor(out=h2t[:], in0=bufA[:], in1=h1t[hs][:], op=ALU.add),
                         reads=["bufA", ("h1t", hs)], writes=["h2t"])
                    S.op("dve", lambda e: e.tensor_copy(out=cb[:], in_=h2t[:]), reads=["h2t"], writes=["cb"])
                    for k in range(8):
                        S.op("pe", lambda e, k=k: e.transpose(out=tpD[:, k, :], in_=cb[:, k * 128:(k + 1) * 128], identity=ident[:]),
                             reads=["cb", "ident"], writes=["tpD"])
                    S.op("act", lambda e: e.copy(out=h2T[:], in_=tpD[:]), reads=["tpD"], writes=["h2T"])
                    gb = yb
                    for half in range(2):
                        for k in range(8):
                            S.op("pe", lambda e, gb=gb, half=half, k=k: e.matmul(
                                big[gb][:, half * 512:(half + 1) * 512], lhsT=h2T[:, k, :],
                                rhs=wpg[:, k, half * 512:(half + 1) * 512], start=(k == 0), stop=(k == 7)),
                                reads=["h2T"] + WPG, writes=[("big", gb)])
                    S.op("dve", lambda e, gb=gb: e.tensor_tensor(out=bufB[:], in0=big[gb][:], in1=bpe[:], op=ALU.add),
                         reads=[("big", gb), "bpe"], writes=["bufB"])
                    S.op("act", lambda e: e.activation(out=bufB[:], in_=bufB[:], func=AF.Sigmoid), reads=["bufB"], writes=["bufB"])
                    pbk = yb
                    for half in range(2):
                        for k in range(2):
                            S.op("pe", lambda e, pbk=pbk, half=half, k=k, t=t: e.matmul(
                                big[pbk][:, half * 512:(half + 1) * 512], lhsT=pT[:, k, t * 128:(t + 1) * 128],
                                rhs=wpp[:, k, half * 512:(half + 1) * 512], start=(k == 0), stop=(k == 1)),
                                reads=[("pT", t)] + WPP, writes=[("big", pbk)])
                    S.op("dve", lambda e, pbk=pbk: e.tensor_tensor(out=bufA[:], in0=big[pbk][:], in1=bufB[:], op=ALU.mult),
                         reads=[("big", pbk), "bufB"], writes=["bufA"])
                    S.op("pool", lambda e: e.tensor_tensor(out=bufB[:], in0=bufA[:], in1=h2t[:], op=ALU.add),
                         reads=["bufA", "h2t"], writes=["bufB"])
                    S.op("gq", lambda e, tok=tok: e.dma_start(out=out[tok:tok + 128, :], in_=bufB[:]),
                         reads=["bufB"], writes=["out"])
            S.op("spw", None, reads=["out"])
            S.flush()

    return nc


def _prep_inputs(inputs):
    f = np.float32
    x = np.asarray(inputs["x"], dtype=f)
    p = np.asarray(inputs["p"], dtype=f)
    shared = {
        "w_in": np.ascontiguousarray(inputs["w_in"][0], dtype=f),
        "w_out": np.ascontiguousarray(inputs["w_out"][0], dtype=f),
        "w_gu": np.ascontiguousarray(inputs["w_gate_up"][0], dtype=f),
        "w_dn": np.ascontiguousarray(inputs["w_down"][0], dtype=f),
        "w_pg": np.ascontiguousarray(inputs["w_pe_gate"][0], dtype=f),
        "w_pp": np.ascontiguousarray(inputs["w_pe_proj"][0], dtype=f),
        "wspT": np.ascontiguousarray(np.transpose(np.asarray(inputs["w_spatial"][0], dtype=f), (2, 0, 1))),
        "bsT": np.ascontiguousarray(np.asarray(inputs["b_spatial"][0], dtype=f).T),
        "g_pre": np.ascontiguousarray(np.asarray(inputs["ln_pre_mix"][0], dtype=f).reshape(8, 128).T),
        "g_out": np.ascontiguousarray(np.concatenate([np.asarray(inputs["attn_out_norm"][0], dtype=f),
                                                      np.asarray(inputs["sgu_out_norm"][0], dtype=f)]).reshape(8, 128).T),
        "g_ffn": np.ascontiguousarray(np.asarray(inputs["ln_pre_ffn"][0], dtype=f).reshape(8, 128).T),
        "lng_bc": np.ascontiguousarray(np.broadcast_to(np.tile(np.asarray(inputs["sgu_ln_g"][0], dtype=f), 4)[None, :], (128, 512))),
        "lnb_bc": np.ascontiguousarray(np.broadcast_to(np.tile(np.asarray(inputs["sgu_ln_b"][0], dtype=f), 4)[None, :], (128, 512))),
        "gpm_bc": np.ascontiguousarray(np.broadcast_to(np.asarray(inputs["ln_post_mix"][0], dtype=f)[None, :], (128, 1024))),
        "gpf_bc": np.ascontiguousarray(np.broadcast_to(np.asarray(inputs["ln_post_ffn"][0], dtype=f)[None, :], (128, 1024))),
        "bpe_bc": np.ascontiguousarray(np.broadcast_to(np.asarray(inputs["b_pe_gate"][0], dtype=f)[None, :], (128, 1024))),
    }
    in_maps = []
    for c in range(8):
        b, half = c // 2, c % 2
        t0 = half * NTOK
        if half == 0:
            halo = np.zeros((NHALO, D_MODEL), dtype=f)
        else:
            halo = x[b, t0 - NHALO:t0]
        m = dict(shared)
        m["xc"] = np.ascontiguousarray(np.concatenate([halo, x[b, t0:t0 + NTOK]], axis=0))
        m["pc"] = np.ascontiguousarray(p[0, b, t0:t0 + NTOK])
        m["flag"] = np.full((128, 64), float(half), dtype=f)
        in_maps.append(m)
    return in_maps


_NC_CACHE = {}


def kernel(**inputs):
    in_maps = _prep_inputs(inputs)
    if "nc" not in _NC_CACHE:
        _NC_CACHE["nc"] = build_nc()
    nc = _NC_CACHE["nc"]
    res = run_bass_kernel_spmd(nc, in_maps, core_ids=list(range(8)))
    outp = np.empty((4, 2 * NTOK, D_MODEL), dtype=np.float32)
    for c in range(8):
        b, half = c // 2, c % 2
        outp[b, half * NTOK:(half + 1) * NTOK] = res.results[c]["out"]
    return outp
```

```python
import numpy as np
from contextlib import ExitStack
import concourse.bass as bass
import concourse.mybir as mybir
from concourse.bass_utils import run_bass_kernel_spmd

F32 = mybir.dt.float32
BF16 = mybir.dt.bfloat16
AF = mybir.ActivationFunctionType
ALU = mybir.AluOpType

NTOK = 4096
NHALO = 2048
NKV = NTOK + NHALO
D_MODEL = 1024
D_FF = 2816
NFF = D_FF // 128
EPS = 1e-6

COMPUTE = ("pe", "act", "dve", "pool")
DMAQ = ("sp", "gq")
NSEM_DMA = 8
class Sched:
    def __init__(self, nc, stack):
        self.nc = nc
        self.ops = []
        self.lastw = {}
        self.readers = {}
        self.emitted = 0
        self.sem = {}
        for e in COMPUTE:
            self.sem[e] = stack.enter_context(nc.semaphore("s_" + e))
        self.dsem = {q: [stack.enter_context(nc.semaphore(f"d_{q}{i}")) for i in range(NSEM_DMA)]
                     for q in DMAQ}
        self.sigcount = {e: 0 for e in COMPUTE}
        self.dcount = {q: 0 for q in DMAQ}
        self.waited = {}
        self.last_op = {}
        self._pending_barrier = {}
        self._dma_ids = {}

    @staticmethod
    def stream(eng):
        return "pool" if eng == "gq" else ("sp" if eng == "spw" else eng)

    def op(self, eng, fn, reads=(), writes=()):
        deps = set(self.barrier_deps_for(eng))
        for r in reads:
            if r in self.lastw:
                deps.add(self.lastw[r])
        for w in writes:
            if w in self.lastw:
                deps.add(self.lastw[w])
            for rd in self.readers.get(w, ()):
                deps.add(rd)
        if eng == "spw":
            for q in DMAQ:
                deps.update(self._dma_ids.get(q, [])[-NSEM_DMA:])
        oid = len(self.ops)
        deps.discard(oid)
        self.ops.append(dict(eng=eng, fn=fn, deps=deps, sig=None))
        for r in reads:
            self.readers.setdefault(r, []).append(oid)
        for w in writes:
            self.lastw[w] = oid
            self.readers[w] = []
        if fn is not None:
            self.last_op[self.stream(eng)] = oid
        if eng in DMAQ:
            self._dma_ids.setdefault(eng, []).append(oid)
        return oid

    def barrier_deps_for(self, eng):
        st = self.stream(eng)
        if st in self._pending_barrier:
            d = self._pending_barrier.pop(st)
            return d
        return ()

    def barrier(self):
        deps = set()
        for st, oid in self.last_op.items():
            deps.add(oid)
        for q in DMAQ:
            ids = self._dma_ids.setdefault(q, [])
            deps.update(ids[-NSEM_DMA:])
        self._pending_barrier = {st: set(deps) for st in ("pe", "act", "dve", "pool", "sp")}

    def flush(self, final=False):
        nc = self.nc
        ops = self.ops
        lo = self.emitted
        hi = len(ops)
        needed = set()
        for i in range(lo, hi):
            o = ops[i]
            for d in o["deps"]:
                if d < lo:
                    continue
                de = ops[d]["eng"]
                if de in COMPUTE:
                    if de == "pe" and o["eng"] == "pe":
                        continue
                    needed.add(d)
        for d in needed:
            assert d >= lo or ops[d]["sig"] is not None, "cross-flush dep on unsignalled op"
        last_in = {}
        for i in range(lo, hi):
            if ops[i]["fn"] is not None:
                last_in[ops[i]["eng"]] = i
        for e, i in last_in.items():
            if e in COMPUTE:
                needed.add(i)
        for i in range(lo, hi):
            o = ops[i]
            e = o["eng"]
            if e in COMPUTE:
                if i in needed:
                    assert o["fn"] is not None
                    self.sigcount[e] += 1
                    o["sig"] = (self.sem[e], self.sigcount[e], ("c", e))
            elif e == "spw":
                pass
            else:
                k = self.dcount[e]
                self.dcount[e] += 1
                o["sig"] = (self.dsem[e][k % NSEM_DMA], 16 * (k // NSEM_DMA + 1), ("d", e, k % NSEM_DMA))
                o["dk"] = k
        per_stream = {st: [] for st in ("pe", "act", "dve", "pool", "sp")}
        for i in range(lo, hi):
            per_stream[self.stream(ops[i]["eng"])].append(i)

        def emit(st, engobj):
            for i in per_stream[st]:
                o = ops[i]
                waits = []
                for d in sorted(o["deps"]):
                    od = ops[d]
                    if od["eng"] == "pe" and o["eng"] == "pe":
                        continue
                    if od["sig"] is None:
                        assert d < lo, (d, lo, od["eng"], o["eng"])
                        continue
                    sem, val, key = od["sig"]
                    waits.append((sem, val, key))
                if o["eng"] in DMAQ:
                    k = o["dk"]
                    if k >= NSEM_DMA:
                        q = o["eng"]
                        waits.append((self.dsem[q][k % NSEM_DMA], 16 * (k // NSEM_DMA), ("d", q, k % NSEM_DMA)))
                for sem, val, key in waits:
                    wk = (st, key)
                    if self.waited.get(wk, 0) >= val:
                        continue
                    self.waited[wk] = val
                    engobj.wait_ge(sem, val)
                if o["fn"] is None:
                    continue
                ins = o["fn"](engobj)
                if o["sig"] is not None and o["eng"] in COMPUTE:
                    ins.then_inc(o["sig"][0], 1)
                elif o["eng"] in DMAQ:
                    ins.then_inc(o["sig"][0], 16)

        indeg = {}
        succ = {}
        def _edge(a, b):
            succ.setdefault(a, []).append(b)
            indeg[b] = indeg.get(b, 0) + 1
        for st_, lst in per_stream.items():
            for a, b in zip(lst, lst[1:]):
                _edge(a, b)
        for i in range(lo, hi):
            indeg.setdefault(i, 0)
            for d in ops[i]["deps"]:
                if d >= lo:
                    _edge(d, i)
        ready = [i for i in range(lo, hi) if indeg[i] == 0]
        seen = 0
        while ready:
            a = ready.pop()
            seen += 1
            for b in succ.get(a, ()):
                indeg[b] -= 1
                if indeg[b] == 0:
                    ready.append(b)
        assert seen == hi - lo, f"schedule deadlock: {hi - lo - seen} ops unreachable"

        with nc.Block() as block:
            @block.sync
            def _(e):
                emit("sp", e)

            @block.scalar
            def _(e):
                emit("act", e)

            @block.vector
            def _(e):
                emit("dve", e)

            @block.gpsimd
            def _(e):
                emit("pool", e)

            @block.tensor
            def _(e):
                emit("pe", e)
        self.emitted = hi
        self.barrier()


def build_nc(debug=False, phases="ABCD"):
    nc = bass.Bass("TRN2", target_bir_lowering=False)

    def din(name, shape, dt=F32):
        return nc.dram_tensor(name, list(shape), dt, kind="ExternalInput").ap()

    xc = din("xc", [NKV, D_MODEL])
    pc = din("pc", [NTOK, 256])
    flag = din("flag", [128, 64])
    w_in = din("w_in", [1024, 2560])
    w_out = din("w_out", [1024, 1024])
    w_gu = din("w_gu", [1024, 2 * D_FF])
    w_dn = din("w_dn", [D_FF, 1024])
    w_pg = din("w_pg", [1024, 1024])
    w_pp = din("w_pp", [256, 1024])
    wspT_in = din("wspT", [128, 4, 128])
    bsT_in = din("bsT", [128, 4])
    g_pre_in = din("g_pre", [128, 8])
    g_out_in = din("g_out", [128, 8])
    g_ffn_in = din("g_ffn", [128, 8])
    lng_in = din("lng_bc", [128, 512])
    lnb_in = din("lnb_bc", [128, 512])
    gpm_in = din("gpm_bc", [128, 1024])
    gpf_in = din("gpf_bc", [128, 1024])
    bpe_in = din("bpe_bc", [128, 1024])

    skind = "ExternalOutput" if debug else "Internal"
    out = nc.dram_tensor("out", [NTOK, D_MODEL], F32, kind="ExternalOutput").ap()
    qT_d = nc.dram_tensor("qT_d", [4, 128, NTOK], BF16, kind=skind).ap()
    kT_d = nc.dram_tensor("kT_d", [4, 128, NKV], BF16, kind=skind).ap()
    v_d = nc.dram_tensor("v_d", [NKV, 512], BF16, kind=skind).ap()
    h1_d = nc.dram_tensor("h1_d", [NTOK, D_MODEL], F32, kind=skind).ap()
    gs_d = nc.dram_tensor("gs_d", [128, 4, NTOK], BF16, kind=skind).ap() if debug else None

    with ExitStack() as top:
        S = Sched(nc, top)

        def alloc(st, name, shape, dt):
            return st.enter_context(nc.sbuf_tensor(name + "_sb", list(shape), dt))

        def palloc(st, name, shape, dt):
            return st.enter_context(nc.psum_tensor(name + "_ps", list(shape), dt))

        ident = alloc(top, "ident", [128, 128], BF16)
        ones_bf = alloc(top, "ones_bf", [128, 128], BF16)
        flag_bf = alloc(top, "flag_bf", [128, 64], BF16)
        gstack = ExitStack()
        gsgu = alloc(gstack, "gsgu", [128, 4, NTOK], BF16)
        with ExitStack() as st:
            tmpf = alloc(st, "c_tmpf", [128, 128], F32)
            flg = alloc(st, "c_flg", [128, 64], F32)
            S.op("pool", lambda e: e.memset(tmpf[:], 1.0), writes=["tmpf"])
            S.op("dve", lambda e: e.tensor_copy(out=ones_bf[:], in_=tmpf[:]), reads=["tmpf"], writes=["ones_bf"])
            S.op("pool", lambda e: e.affine_select(out=tmpf[:], in_=tmpf[:], pattern=[[1, 128]],
                                                   compare_op=ALU.is_equal, fill=0.0, base=0,
                                                   channel_multiplier=-1), reads=["tmpf"], writes=["tmpf"])
            S.op("dve", lambda e: e.tensor_copy(out=ident[:], in_=tmpf[:]), reads=["tmpf"], writes=["ident"])
            S.op("sp", lambda e: e.dma_start(out=flg[:], in_=flag[:, :]), writes=["flg"])
            S.op("dve", lambda e: e.tensor_copy(out=flag_bf[:], in_=flg[:]), reads=["flg"], writes=["flag_bf"])
            S.flush()

        if "A" in phases:
          with ExitStack() as st:
            w_in_bf = alloc(st, "w_in_bf", [128, 8, 2560], BF16)
            wstage = [alloc(st, f"wstage{i}", [128, 2560], F32) for i in range(2)]
            g_pre = alloc(st, "g_pre_sb", [128, 8], F32)
            wsp_f = alloc(st, "wsp_f", [128, 4, 128], F32)
            wspT = alloc(st, "wspT", [128, 4, 128], BF16)
            bsT = alloc(st, "bsT", [128, 4], F32)
            lng = alloc(st, "lng", [128, 512], F32)
            lnb = alloc(st, "lnb", [128, 512], F32)
            xs = [alloc(st, f"xs{i}", [128, 4, 1024], F32) for i in range(2)]
            junk = alloc(st, "junkA", [128, 1024], BF16)
            ssq = [alloc(st, f"ssqA{i}", [128, 4], F32) for i in range(2)]
            rsq = [alloc(st, f"rsqA{i}", [128, 4], F32) for i in range(2)]
            ab = [alloc(st, f"abA{i}", [128, 1024], BF16) for i in range(2)]
            aT = [alloc(st, f"aTA{i}", [128, 8, 512], BF16) for i in range(2)]
            kqst = [alloc(st, f"kqst{i}", [128, 512], BF16) for i in range(4)]
            vst = [alloc(st, f"vst{i}", [128, 512], BF16) for i in range(2)]
            guz = alloc(st, "guz", [128, 1024], F32)
            st6 = alloc(st, "st6", [128, 4, 6], F32)
            mv = alloc(st, "mv", [128, 4, 2], F32)
            rstd4 = alloc(st, "rstd4", [128, 4], F32)
            nb4 = alloc(st, "nb4", [128, 4], F32)
            zn = alloc(st, "zn", [128, 512], F32)
            znb = alloc(st, "znb", [128, 512], BF16)
            sg = alloc(st, "sg", [128, 512], F32)
            ss2 = alloc(st, "ss2", [128, 1], F32)
            rs2 = alloc(st, "rs2", [128, 1], F32)
            sgn = alloc(st, "sgn", [128, 512], BF16)
            tp = [palloc(st, f"tpA{i}", [128, 8, 128], BF16) for i in range(2)]
            kq = [palloc(st, f"kqA{i}", [128, 512], F32) for i in range(2)]
            uz = palloc(st, "uzA", [128, 1024], F32)
            vp = palloc(st, "vpA", [128, 512], F32)
            mx = palloc(st, "mxA", [128, 512], F32)

            S.op("sp", lambda e: e.dma_start(out=g_pre[:], in_=g_pre_in[:, :]), writes=["g_pre"])
            S.op("sp", lambda e: e.dma_start(out=wsp_f[:], in_=wspT_in[:, :, :]), writes=["wsp_f"])
            S.op("sp", lambda e: e.dma_start(out=bsT[:], in_=bsT_in[:, :]), writes=["bsT"])
            S.op("sp", lambda e: e.dma_start(out=lng[:], in_=lng_in[:, :]), writes=["lng"])
            S.op("sp", lambda e: e.dma_start(out=lnb[:], in_=lnb_in[:, :]), writes=["lnb"])
            S.op("pool", lambda e: e.affine_select(out=wsp_f[:], in_=wsp_f[:], pattern=[[0, 4], [1, 128]],
                                                   compare_op=ALU.is_ge, fill=0.0, base=0,
                                                   channel_multiplier=-1), reads=["wsp_f"], writes=["wsp_f"])
            S.op("dve", lambda e: e.tensor_copy(out=wspT[:], in_=wsp_f[:]), reads=["wsp_f"], writes=["wspT"])
            for k in range(8):
                s = k % 2
                S.op("sp", lambda e, k=k, s=s: e.dma_start(out=wstage[s][:], in_=w_in[k * 128:(k + 1) * 128, :]),
                     writes=[("wstage", s)])
                if k % 2 == 0:
                    S.op("act", lambda e, k=k, s=s: e.activation(out=w_in_bf[:, k, :], in_=wstage[s][:], func=AF.Identity,
                                                                 scale=g_pre[:, k:k + 1]),
                         reads=[("wstage", s), "g_pre"], writes=[("w_in_bf", k)])
                else:
                    S.op("dve", lambda e, k=k, s=s: e.tensor_scalar(out=w_in_bf[:, k, :], in0=wstage[s][:],
                                                                    scalar1=g_pre[:, k:k + 1], scalar2=None, op0=ALU.mult),
                         reads=[("wstage", s), "g_pre"], writes=[("w_in_bf", k)])
            WIN = [("w_in_bf", k) for k in range(8)]

            nkq = 0
            nv = 0
            for bi in range(12):
                s = bi % 2
                main = bi >= 4
                S.op("sp", lambda e, bi=bi, s=s: e.dma_start(
                    out=xs[s][:], in_=xc[bi * 512:(bi + 1) * 512, :].rearrange("(t p) d -> p t d", p=128)),
                    writes=[("xs", s)])
                for t in range(4):
                    S.op("act", lambda e, s=s, t=t: e.activation(out=junk[:], in_=xs[s][:, t, :], func=AF.Square,
                                                                 accum_out=ssq[s][:, t:t + 1]),
                         reads=[("xs", s)], writes=["junkA", ("ssq", s, t)])
                S.op("act", lambda e, s=s: e.activation(out=rsq[s][:], in_=ssq[s][:], func=AF.Ln, scale=1.0 / 1024, bias=EPS),
                     reads=[("ssq", s, t) for t in range(4)], writes=[("rsq", s)])
                S.op("act", lambda e, s=s: e.activation(out=rsq[s][:], in_=rsq[s][:], func=AF.Exp, scale=-0.5),
                     reads=[("rsq", s)], writes=[("rsq", s)])
                for t in range(4):
                    a = t % 2
                    S.op("dve", lambda e, s=s, t=t, a=a: e.tensor_scalar(out=ab[a][:], in0=xs[s][:, t, :],
                                                                         scalar1=rsq[s][:, t:t + 1], scalar2=None,
                                                                         op0=ALU.mult),
                         reads=[("xs", s), ("rsq", s)], writes=[("ab", a)])
                    for k in range(8):
                        S.op("pe", lambda e, a=a, k=k: e.transpose(out=tp[a][:, k, :], in_=ab[a][:, k * 128:(k + 1) * 128],
                                                                   identity=ident[:]),
                             reads=[("ab", a), "ident"], writes=[("tp", a)])
                    if t % 2 == 0:
                        S.op("act", lambda e, s=s, t=t, a=a: e.copy(out=aT[s][:, :, t * 128:(t + 1) * 128], in_=tp[a][:]),
                             reads=[("tp", a)], writes=[("aT", s, t)])
                    else:
                        S.op("dve", lambda e, s=s, t=t, a=a: e.tensor_copy(out=aT[s][:, :, t * 128:(t + 1) * 128], in_=tp[a][:]),
                             reads=[("tp", a)], writes=[("aT", s, t)])
                AT = [("aT", s, t) for t in range(4)]
                for which in (("k", "q") if main else ("k",)):
                    for c in range(4):
                        pz = nkq % 2
                        ks_ = nkq % 4
                        nkq += 1
                        col0 = (512 if which == "k" else 0) + c * 128
                        for k in range(8):
                            S.op("pe", lambda e, pz=pz, k=k, col0=col0, s=s: e.matmul(
                                kq[pz][:], lhsT=w_in_bf[:, k, col0:col0 + 128], rhs=aT[s][:, k, :],
                                start=(k == 0), stop=(k == 7)),
                                reads=AT + WIN, writes=[("kq", pz)])
                        if which == "k":
                            S.op("act", lambda e, pz=pz, ks_=ks_: e.copy(out=kqst[ks_][:], in_=kq[pz][:]),
                                 reads=[("kq", pz)], writes=[("kqst", ks_)])
                            S.op("gq", lambda e, ks_=ks_, c=c, bi=bi: e.dma_start(
                                out=kT_d[c, :, bi * 512:(bi + 1) * 512], in_=kqst[ks_][:]),
                                reads=[("kqst", ks_)], writes=["kT_d"])
                        else:
                            S.op("dve", lambda e, pz=pz, ks_=ks_: e.tensor_scalar(out=kqst[ks_][:], in0=kq[pz][:],
                                                                                  scalar1=0.125, scalar2=None,
                                                                                  op0=ALU.mult),
                                 reads=[("kq", pz)], writes=[("kqst", ks_)])
                            S.op("gq", lambda e, ks_=ks_, c=c, bi=bi: e.dma_start(
                                out=qT_d[c, :, (bi - 4) * 512:(bi - 3) * 512], in_=kqst[ks_][:]),
                                reads=[("kqst", ks_)], writes=["qT_d"])
                for t in range(4):
                    vs_ = nv % 2
                    nv += 1
                    for k in range(8):
                        S.op("pe", lambda e, k=k, s=s, t=t: e.matmul(
                            vp[:], lhsT=aT[s][:, k, t * 128:(t + 1) * 128], rhs=w_in_bf[:, k, 1024:1536],
                            start=(k == 0), stop=(k == 7)),
                            reads=AT + WIN, writes=["vp"])
                    S.op("act", lambda e, vs_=vs_: e.copy(out=vst[vs_][:], in_=vp[:]), reads=["vp"], writes=[("vst", vs_)])
                    S.op("gq", lambda e, vs_=vs_, bi=bi, t=t: e.dma_start(
                        out=v_d[bi * 512 + t * 128: bi * 512 + (t + 1) * 128, :], in_=vst[vs_][:]),
                        reads=[("vst", vs_)], writes=["v_d"])
                    if not main:
                        continue
                    tok = (bi - 4) * 512 + t * 128
                    for hf in range(2):
                        for k in range(8):
                            S.op("pe", lambda e, k=k, s=s, t=t, hf=hf: e.matmul(
                                uz[:, hf * 512:(hf + 1) * 512], lhsT=aT[s][:, k, t * 128:(t + 1) * 128],
                                rhs=w_in_bf[:, k, 1536 + hf * 512:2048 + hf * 512], start=(k == 0), stop=(k == 7)),
                                reads=AT + WIN, writes=["uz"])
                    S.op("act", lambda e: e.activation(out=guz[:], in_=uz[:], func=AF.Gelu_apprx_tanh),
                         reads=["uz"], writes=["guz"])
                    for g in range(4):
                        S.op("dve", lambda e, g=g: e.bn_stats(out=st6[:, g, :], in_=guz[:, 512 + g * 128:512 + (g + 1) * 128]),
                             reads=["guz"], writes=[("st6", g)])
                    for g in range(4):
                        S.op("dve", lambda e, g=g: e.bn_aggr(out=mv[:, g, :], in_=st6[:, g, :]),
                             reads=[("st6", g)], writes=[("mv", g)])
                    MV = [("mv", g) for g in range(4)]
                    S.op("act", lambda e: e.activation(out=rstd4[:], in_=mv[:, :, 1], func=AF.Ln, bias=EPS),
                         reads=MV, writes=["rstd4"])
                    S.op("act", lambda e: e.activation(out=rstd4[:], in_=rstd4[:], func=AF.Exp, scale=-0.5),
                         reads=["rstd4"], writes=["rstd4"])
                    S.op("dve", lambda e: e.scalar_tensor_tensor(out=nb4[:], in0=mv[:, :, 0], scalar=-1.0, in1=rstd4[:],
                                                                 op0=ALU.mult, op1=ALU.mult),
                         reads=MV + ["rstd4"], writes=["nb4"])
                    for g in range(4):
                        S.op("dve", lambda e, g=g: e.tensor_scalar(out=zn[:, g * 128:(g + 1) * 128],
                                                                   in0=guz[:, 512 + g * 128:512 + (g + 1) * 128],
                                                                   scalar1=rstd4[:, g:g + 1], scalar2=nb4[:, g:g + 1],
                                                                   op0=ALU.mult, op1=ALU.add),
                             reads=["guz", "rstd4", "nb4"], writes=[("zn", g)])
                    ZN = [("zn", g) for g in range(4)]
                    S.op("dve", lambda e: e.tensor_tensor(out=zn[:], in0=zn[:], in1=lng[:], op=ALU.mult),
                         reads=ZN + ["lng"], writes=ZN)
                    S.op("dve", lambda e: e.tensor_tensor(out=znb[:], in0=zn[:], in1=lnb[:], op=ALU.add),
                         reads=ZN + ["lnb"], writes=["znb"])
                    for g in range(4):
                        S.op("pe", lambda e, g=g: e.matmul(mx[:, g * 128:(g + 1) * 128], lhsT=wspT[:, g, :],
                                                           rhs=znb[:, g * 128:(g + 1) * 128], start=True, stop=True),
                             reads=["znb", "wspT"], writes=["mx"])
                    for g in range(4):
                        S.op("dve", lambda e, g=g: e.scalar_tensor_tensor(
                            out=sg[:, g * 128:(g + 1) * 128], in0=mx[:, g * 128:(g + 1) * 128], scalar=bsT[:, g:g + 1],
                            in1=guz[:, g * 128:(g + 1) * 128], op0=ALU.add, op1=ALU.mult),
                            reads=["mx", "guz", "bsT"], writes=[("sg", g)])
                    SG = [("sg", g) for g in range(4)]
                    S.op("act", lambda e: e.activation(out=junk[:, 0:512], in_=sg[:], func=AF.Square, accum_out=ss2[:]),
                         reads=SG, writes=["junkA", "ss2"])
                    S.op("act", lambda e: e.activation(out=rs2[:], in_=ss2[:], func=AF.Ln, scale=1.0 / 512, bias=EPS),
                         reads=["ss2"], writes=["rs2"])
                    S.op("act", lambda e: e.activation(out=rs2[:], in_=rs2[:], func=AF.Exp, scale=-0.5),
                         reads=["rs2"], writes=["rs2"])
                    S.op("dve", lambda e: e.tensor_scalar(out=sgn[:], in0=sg[:], scalar1=rs2[:, 0:1], scalar2=None,
                                                          op0=ALU.mult),
                         reads=SG + ["rs2"], writes=["sgn"])
                    a2 = t % 2
                    for g in range(4):
                        S.op("pe", lambda e, g=g, a2=a2: e.transpose(out=tp[a2][:, g, :], in_=sgn[:, g * 128:(g + 1) * 128],
                                                                     identity=ident[:]),
                             reads=["sgn", "ident"], writes=[("tp", a2)])
                    S.op("dve", lambda e, a2=a2, tok=tok: e.tensor_copy(out=gsgu[:, :, tok:tok + 128], in_=tp[a2][:, 0:4, :]),
                         reads=[("tp", a2)], writes=[("gsgu", tok // 128)])
            if debug:
                S.op("sp", lambda e: e.dma_start(out=gs_d[:, :, :], in_=gsgu[:]),
                     reads=[("gsgu", i) for i in range(32)], writes=["gs_d"])
            S.flush()

        if "B" in phases:
          with ExitStack() as st:
            w_out_bf = alloc(st, "w_out_bf", [128, 8, 1024], BF16)
            g_out = alloc(st, "g_outB", [128, 8], F32)
            gpm = alloc(st, "gpmB", [128, 1024], F32)
            Mk = [[alloc(st, f"Mk{j}_{di}", [128, 2, 2, 128], BF16) for di in range(3)] for j in range(4)]
            iot = alloc(st, "iotB", [128, 2, 128], F32)
            mtmp = alloc(st, "mtmpB", [128, 2, 128], F32)
            qs = [alloc(st, f"qsB{i}", [128, 2048], BF16) for i in range(2)]
            ks = [alloc(st, f"ksB{i}", [128, 4096], BF16) for i in range(2)]
            V1 = [alloc(st, f"V1B{i}", [128, 17, 128], BF16) for i in range(2)]
            V4 = [alloc(st, f"V4B{i}", [128, 5, 4, 128], BF16) for i in range(2)]
            V16 = [alloc(st, f"V16B{i}", [128, 2, 16, 128], BF16) for i in range(2)]
            acc = alloc(st, "accB", [128, 2, 2048], F32)
            wstg = [acc[:, i, 0:1024] for i in range(2)]
            outT = alloc(st, "outTB", [128, 4, 2048], F32)
            Eb = [alloc(st, f"EbB{i}", [128, 512], BF16) for i in range(2)]
            Em = [alloc(st, f"EmB{i}", [128, 2, 2, 128], BF16) for i in range(3)]
            sq = alloc(st, "sqB", [128, 4, 512], BF16)
            rb = alloc(st, "rbB", [128, 512], F32)
            gTa = alloc(st, "gTaB", [128, 4, 512], BF16)
            xt = [alloc(st, f"xtB{i}", [128, 1024], F32) for i in range(2)]
            junkB = alloc(st, "junkB", [128, 1024], BF16)
            ss1 = alloc(st, "ss1B", [128, 1], F32)
            rs1 = alloc(st, "rs1B", [128, 1], F32)
            t1 = alloc(st, "t1B", [128, 1024], F32)
            sc = [palloc(st, f"scB{i}", [128, 2, 512], F32) for i in range(3)]
            pvb = [palloc(st, f"pvB{i}", [128, 512], F32) for i in range(2)]
            pv = [pvb[i][:, 0:256].rearrange("p (a q) -> p a q", a=2) for i in range(2)]
            ssp = pvb[0]
            mixed = sc[0][:, :, :].rearrange("p h c -> p (h c)")

            S.op("sp", lambda e: e.dma_start(out=g_out[:], in_=g_out_in[:, :]), writes=["g_out"])
            S.op("sp", lambda e: e.dma_start(out=gpm[:], in_=gpm_in[:, :]), writes=["gpm"])
            for k in range(8):
                s = k % 2
                S.op("sp", lambda e, k=k, s=s: e.dma_start(out=wstg[s], in_=w_out[k * 128:(k + 1) * 128, :]),
                     writes=[("wstg", s)])
                S.op("dve", lambda e, k=k, s=s: e.tensor_scalar(out=w_out_bf[:, k, :], in0=wstg[s],
                                                                scalar1=g_out[:, k:k + 1], scalar2=None, op0=ALU.mult),
                     reads=[("wstg", s), "g_out"], writes=[("w_out_bf", k)])
            WOUT = [("w_out_bf", k) for k in range(8)]
            S.op("pool", lambda e: e.iota(iot[:, 0, :], pattern=[[1, 128]], base=128, channel_multiplier=-1,
                                          allow_small_or_imprecise_dtypes=True), writes=["iot0"])
            S.op("pool", lambda e: e.iota(iot[:, 1, :], pattern=[[1, 128]], base=0, channel_multiplier=-1,
                                          allow_small_or_imprecise_dtypes=True), writes=["iot1"])
            S.op("dve", lambda e: e.tensor_scalar(out=iot[:], in0=iot[:], scalar1=0.0, scalar2=None, op0=ALU.max),
                 reads=["iot0", "iot1"], writes=["iot"])
            for j in range(4):
                for di, D in enumerate((1, 4, 16)):
                    for hl in range(2):
                        h = 2 * j + hl
                        slope = 2.0 ** (-(h + 1))
                        S.op("act", lambda e, slope=slope, D=D: e.activation(out=mtmp[:], in_=iot[:], func=AF.Exp,
                                                                             scale=-slope * D),
                             reads=["iot"], writes=["mtmp"])
                        S.op("pool", lambda e, j=j, di=di, hl=hl: e.affine_select(
                            out=Mk[j][di][:, hl, 0, :], in_=mtmp[:, 0, :], pattern=[[-1, 128]], compare_op=ALU.is_ge,
                            fill=0.0, base=0, channel_multiplier=1), reads=["mtmp"], writes=[("Mk", j, di, hl, 0)])
                        S.op("pool", lambda e, j=j, di=di, hl=hl: e.affine_select(
                            out=Mk[j][di][:, hl, 1, :], in_=mtmp[:, 1, :], pattern=[[1, 128]], compare_op=ALU.is_ge,
                            fill=0.0, base=0, channel_multiplier=-1), reads=["mtmp"], writes=[("Mk", j, di, hl, 1)])

            def accres(D, nd, r):
                if D == 16:
                    return [("acc", t, r) for t in range(16)]
                if D == 4:
                    return [("acc", nd * 4 + t, r + 4 * m) for t in range(4) for m in range(4)]
                return [("acc", nd, m) for m in range(16)]

            ALLACC = [("acc", t, m) for t in range(16) for m in range(16)]
            nblk = 0
            nit = 0
            nx = 0
            import os
            NSPAN = int(os.environ.get("NSPAN", "2")); NJ = int(os.environ.get("NJ", "4")); NDIL = int(os.environ.get("NDIL", "3")); NND = int(os.environ.get("NND", "99")); NOC = int(os.environ.get("NOC", "0"))
            for n in range(NSPAN):
                S0 = NHALO + n * 2048
                for j in range(NJ):
                    sl = nit % 2
                    nit += 1
                    jc = slice(j * 128, (j + 1) * 128)
                    S.op("sp", lambda e, sl=sl, j=j, n=n: e.dma_start(out=qs[sl][:], in_=qT_d[j, :, n * 2048:(n + 1) * 2048]),
                         reads=["qT_d"], writes=[("qs", sl)])
                    S.op("sp", lambda e, sl=sl, j=j, S0=S0: e.dma_start(out=ks[sl][:], in_=kT_d[j, :, S0 - 2048:S0 + 2048]),
                         reads=["kT_d"], writes=[("ks", sl)])
                    for t0 in range(0, 17, 5):
                        t1_ = min(17, t0 + 5)
                        S.op("sp", lambda e, sl=sl, jc=jc, S0=S0, t0=t0, t1_=t1_: e.dma_start(
                            out=V1[sl][:, t0:t1_, :],
                            in_=v_d[S0 - 128 + t0 * 128:S0 - 128 + t1_ * 128, jc].rearrange("(t p) c -> p t c", p=128)),
                            reads=["v_d"], writes=[("V1", sl, t0)])
                    for n4 in range(5):
                        S.op("sp", lambda e, sl=sl, jc=jc, S0=S0, n4=n4: e.dma_start(
                            out=V4[sl][:, n4, :, :],
                            in_=v_d[S0 - 512 + n4 * 512:S0 + n4 * 512, jc].rearrange("(p r) c -> p r c", r=4)),
                            reads=["v_d"], writes=[("V4", sl, n4)])
                    for n16 in range(2):
                        for r0 in range(0, 16, 4):
                            S.op("sp", lambda e, sl=sl, jc=jc, S0=S0, n16=n16, r0=r0: e.dma_start(
                                out=V16[sl][:, n16, r0:r0 + 4, :],
                                in_=v_d[S0 - 2048 + n16 * 2048:S0 + n16 * 2048, jc].rearrange(
                                    "(p r) c -> p r c", r=16)[:, r0:r0 + 4, :]),
                                reads=["v_d"], writes=[("V16", sl, n16, r0)])
                    LOADS = ([("qs", sl), ("ks", sl)] + [("V1", sl, t0) for t0 in range(0, 17, 5)]
                             + [("V4", sl, n4) for n4 in range(5)]
                             + [("V16", sl, a, b) for a in range(2) for b in range(0, 16, 4)])
                    units = []
                    for di, D in ((2, 16), (1, 4), (0, 1))[:NDIL]:
                        for nd in range(min(16 // D, NND)):
                            for r in range(D):
                                base = nd * 128 * D + r
                                qsl = slice(base, base + 127 * D + 1, D)
                                kp = slice(2048 + base - 128 * D, 2048 + base - 128 * D + 127 * D + 1, D)
                                kc = slice(2048 + base, 2048 + base + 127 * D + 1, D)
                                if D == 16:
                                    Vp, Vc = V16[sl][:, 0, r, :], V16[sl][:, 1, r, :]
                                elif D == 4:
                                    Vp, Vc = V4[sl][:, nd, r, :], V4[sl][:, nd + 1, r, :]
                                else:
                                    Vp, Vc = V1[sl][:, nd, :], V1[sl][:, nd + 1, :]
                                halo_prev = (n == 0 and nd == 0)
                                b2 = nblk % 2
                                b3 = nblk % 3
                                bs = nblk % 3
                                nblk += 1
                                def front(b2=b2, b3=b3, bs=bs, kp=kp, kc=kc, qsl=qsl, sl=sl, j=j, di=di, LOADS=LOADS):
                                  for hl in range(2):
                                    hp = slice(hl * 64, (hl + 1) * 64)
                                    for kb, ksl in ((0, kp), (1, kc)):
                                        S.op("pe", lambda e, bs=bs, hl=hl, kb=kb, hp=hp, ksl=ksl, qsl=qsl, sl=sl: e.matmul(
                                            sc[bs][:, hl, kb * 128:(kb + 1) * 128], lhsT=ks[sl][hp, ksl], rhs=qs[sl][hp, qsl],
                                            start=True, stop=True),
                                            reads=LOADS, writes=[("sc", bs)])
                                  S.op("act", lambda e, b2=b2, bs=bs: e.activation(out=Eb[b2][:].rearrange("p (h c) -> p h c", h=2), in_=sc[bs][:, :, 0:256], func=AF.Exp),
                                     reads=[("sc", bs)], writes=[("Eb", b2)])
                                  S.op("dve", lambda e, b2=b2, b3=b3, j=j, di=di: e.tensor_tensor(
                                    out=Em[b3][:], in0=Eb[b2][:].rearrange("p (h k q) -> p h k q", h=2, k=2), in1=Mk[j][di][:], op=ALU.mult),
                                    reads=[("Eb", b2)] + [("Mk", j, di, a, b) for a in range(2) for b in range(2)],
                                    writes=[("Em", b3)])

                                def back(b2=b2, b3=b3, Vp=Vp, Vc=Vc, halo_prev=halo_prev, D=D, nd=nd, r=r, qsl=qsl, LOADS=LOADS):
                                  for hl in range(2):
                                    hp = slice(hl * 64, (hl + 1) * 64)
                                    onesp = flag_bf[:, :] if halo_prev else ones_bf[:, 0:64]
                                    S.op("pe", lambda e, b2=b2, b3=b3, hl=hl, hp=hp, Vp=Vp: e.matmul(
                                        pv[b2][hp, 0, :], lhsT=Vp[:, hp], rhs=Em[b3][:, hl, 0, :], start=True, stop=False),
                                        reads=LOADS + [("Em", b3)], writes=[("pv", b2)])
                                    S.op("pe", lambda e, b2=b2, b3=b3, hl=hl, hp=hp, Vc=Vc: e.matmul(
                                        pv[b2][hp, 0, :], lhsT=Vc[:, hp], rhs=Em[b3][:, hl, 1, :], start=False, stop=True),
                                        reads=LOADS + [("Em", b3)], writes=[("pv", b2)])
                                    S.op("pe", lambda e, b2=b2, b3=b3, hl=hl, hp=hp, onesp=onesp: e.matmul(
                                        pv[b2][hp, 1, :], lhsT=onesp, rhs=Em[b3][:, hl, 0, :], start=True, stop=False),
                                        reads=[("Em", b3), "flag_bf", "ones_bf"], writes=[("pv", b2)])
                                    S.op("pe", lambda e, b2=b2, b3=b3, hl=hl, hp=hp: e.matmul(
                                        pv[b2][hp, 1, :], lhsT=ones_bf[:, 0:64], rhs=Em[b3][:, hl, 1, :], start=False, stop=True),
                                        reads=[("Em", b3), "ones_bf"], writes=[("pv", b2)])
                                  AR = accres(D, nd, r)
                                  if D == 16:
                                    S.op("act", lambda e, b2=b2, qsl=qsl: e.copy(out=acc[:, :, qsl], in_=pv[b2]),
                                         reads=[("pv", b2)], writes=AR)
                                  else:
                                    S.op("dve", lambda e, b2=b2, qsl=qsl: e.tensor_tensor(
                                        out=acc[:, :, qsl], in0=pv[b2], in1=acc[:, :, qsl], op=ALU.add),
                                        reads=[("pv", b2)] + AR, writes=AR)
                                units.append((front, back))
                    for ui, (fr, bk) in enumerate(units):
                        fr()
                        if ui >= 2:
                            units[ui - 2][1]()
                    for bk in [u[1] for u in units[-2:]]:
                        bk()
                    S.op("act", lambda e: e.activation(out=acc[:, 1, :], in_=acc[:, 1, :], func=AF.Ln), reads=ALLACC, writes=ALLACC)
                    S.op("act", lambda e: e.activation(out=acc[:, 1, :], in_=acc[:, 1, :], func=AF.Exp, scale=-1.0),
                         reads=ALLACC, writes=ALLACC)
                    S.op("dve", lambda e, j=j: e.tensor_tensor(out=outT[:, j, :], in0=acc[:, 0, :], in1=acc[:, 1, :], op=ALU.mult),
                         reads=ALLACC, writes=[("outT", j)])
                OUTT = [("outT", j) for j in range(4)]
                for b in range(0 if NOC else 4):
                    cs = slice(b * 512, (b + 1) * 512)
                    for j in range(4):
                        S.op("act", lambda e, j=j, cs=cs: e.activation(out=sq[:, j, :], in_=outT[:, j, cs], func=AF.Square),
                             reads=OUTT, writes=[("sq", j)])
                    for j in range(4):
                        S.op("pe", lambda e, j=j: e.matmul(ssp[:], lhsT=ones_bf[:], rhs=sq[:, j, :], start=(j == 0), stop=(j == 3)),
                             reads=[("sq", j), "ones_bf"], writes=[("pv", 0)])
                    S.op("act", lambda e: e.activation(out=rb[:], in_=ssp[:], func=AF.Ln, scale=1.0 / 512, bias=EPS),
                         reads=[("pv", 0)], writes=["rb"])
                    S.op("act", lambda e: e.activation(out=rb[:], in_=rb[:], func=AF.Exp, scale=-0.5), reads=["rb"], writes=["rb"])
                    for j in range(4):
                        S.op("dve", lambda e, j=j, cs=cs: e.tensor_tensor(out=gTa[:, j, :], in0=outT[:, j, cs], in1=rb[:], op=ALU.mult),
                             reads=OUTT + ["rb"], writes=[("gTa", j)])
                    GTA = [("gTa", j) for j in range(4)]
                    for t in range(4):
                        tok = n * 2048 + b * 512 + t * 128
                        xsl = nx % 2
                        nx += 1
                        S.op("sp", lambda e, xsl=xsl, tok=tok: e.dma_start(out=xt[xsl][:], in_=xc[NHALO + tok:NHALO + tok + 128, :]),
                             writes=[("xt", xsl)])
                        for hf in range(2):
                            for k in range(8):
                                if k < 4:
                                    lh = gTa[:, k, t * 128:(t + 1) * 128]
                                else:
                                    lh = gsgu[:, k - 4, tok:tok + 128]
                                S.op("pe", lambda e, lh=lh, k=k, hf=hf: e.matmul(
                                    mixed[:, hf * 512:(hf + 1) * 512], lhsT=lh, rhs=w_out_bf[:, k, hf * 512:(hf + 1) * 512],
                                    start=(k == 0), stop=(k == 7)),
                                    reads=GTA + WOUT + [("gsgu", tok // 128)], writes=[("sc", 0)])
                        S.op("act", lambda e: e.activation(out=junkB[:], in_=mixed, func=AF.Square, accum_out=ss1[:]),
                             reads=[("sc", 0)], writes=["junkB", "ss1"])
                        S.op("act", lambda e: e.activation(out=rs1[:], in_=ss1[:], func=AF.Ln, scale=1.0 / 1024, bias=EPS),
                             reads=["ss1"], writes=["rs1"])
                        S.op("act", lambda e: e.activation(out=rs1[:], in_=rs1[:], func=AF.Exp, scale=-0.5),
                             reads=["rs1"], writes=["rs1"])
                        S.op("dve", lambda e: e.scalar_tensor_tensor(out=t1[:], in0=mixed, scalar=rs1[:, 0:1], in1=gpm[:],
                                                                     op0=ALU.mult, op1=ALU.mult),
                             reads=[("sc", 0), "rs1", "gpm"], writes=["t1"])
                        S.op("pool", lambda e, xsl=xsl: e.tensor_tensor(out=xt[xsl][:], in0=t1[:], in1=xt[xsl][:], op=ALU.add),
                             reads=["t1", ("xt", xsl)], writes=[("xt", xsl)])
                        S.op("gq", lambda e, xsl=xsl, tok=tok: e.dma_start(out=h1_d[tok:tok + 128, :], in_=xt[xsl][:]),
                             reads=[("xt", xsl)], writes=["h1_d"])
            S.flush()

        gstack.close()
        if "D" in phases:
          with ExitStack() as st:
            wgu = alloc(st, "wgu_bf", [128, 8, 2 * D_FF], BF16)
            wdn = alloc(st, "wdn_bf", [128, NFF, 1024], BF16)
            wpg = alloc(st, "wpg_bf", [128, 8, 1024], BF16)
            wpp = alloc(st, "wpp_bf", [128, 2, 1024], BF16)
            g_ffn = alloc(st, "g_ffnD", [128, 8], F32)
            NB = 256
            with ExitStack() as st2:
                stg = [alloc(st2, f"stgD{i}", [128, 1408], F32) for i in range(3)]
                S.op("sp", lambda e: e.dma_start(out=g_ffn[:], in_=g_ffn_in[:, :]), writes=["g_ffn"])
                pieces = []
                for k in range(8):
                    for q4 in range(4):
                        pieces.append(("gu", k, q4))
                for c in range(NFF):
                    pieces.append(("dn", c, 0))
                for k in range(8):
                    pieces.append(("pg", k, 0))
                for k in range(2):
                    pieces.append(("pp", k, 0))
                for i, (kind, k, q4) in enumerate(pieces):
                    s = i % 3
                    if kind == "gu":
                        src = w_gu[k * 128:(k + 1) * 128, q4 * 1408:(q4 + 1) * 1408]
                        dst = wgu[:, k, q4 * 1408:(q4 + 1) * 1408]
                        w = 1408
                    elif kind == "dn":
                        src = w_dn[k * 128:(k + 1) * 128, :]
                        dst = wdn[:, k, :]
                        w = 1024
                    elif kind == "pg":
                        src = w_pg[k * 128:(k + 1) * 128, :]
                        dst = wpg[:, k, :]
                        w = 1024
                    else:
                        src = w_pp[k * 128:(k + 1) * 128, :]
                        dst = wpp[:, k, :]
                        w = 1024
                    S.op("sp", lambda e, s=s, src=src, w=w: e.dma_start(out=stg[s][:, 0:w], in_=src), writes=[("stg", s)])
                    eng = ("act", "dve")[i % 2]
                    if kind == "gu":
                        if eng == "act":
                            S.op("act", lambda e, s=s, dst=dst, w=w, k=k: e.activation(out=dst, in_=stg[s][:, 0:w], func=AF.Identity,
                                                                                       scale=g_ffn[:, k:k + 1]),
                                 reads=[("stg", s), "g_ffn"], writes=[("wD", i)])
                        else:
                            S.op(eng, lambda e, s=s, dst=dst, w=w, k=k: e.tensor_scalar(out=dst, in0=stg[s][:, 0:w],
                                                                                        scalar1=g_ffn[:, k:k + 1], scalar2=None,
                                                                                        op0=ALU.mult),
                                 reads=[("stg", s), "g_ffn"], writes=[("wD", i)])
                    else:
                        if eng == "act":
                            S.op("act", lambda e, s=s, dst=dst, w=w: e.copy(out=dst, in_=stg[s][:, 0:w]),
                                 reads=[("stg", s)], writes=[("wD", i)])
                        else:
                            S.op(eng, lambda e, s=s, dst=dst, w=w: e.tensor_copy(out=dst, in_=stg[s][:, 0:w]),
                                 reads=[("stg", s)], writes=[("wD", i)])
                S.flush()
            gpf = alloc(st, "gpfD", [128, 1024], F32)
            bpe = alloc(st, "bpeD", [128, 1024], F32)
            fT = alloc(st, "fTD", [128, 8, NB], BF16)
            pT = alloc(st, "pTD", [128, 2, NB], BF16)
            actT = alloc(st, "actTD", [128, NFF, NB], BF16)
            h2T = alloc(st, "h2TD", [128, 8, 128], BF16)
            h1t = [alloc(st, f"h1tD{i}", [128, 1024], F32) for i in range(2)]
            bufA = alloc(st, "bufAD", [128, 1024], F32)
            bufB = alloc(st, "bufBD", [128, 1024], F32)
            h2t = alloc(st, "h2tD", [128, 1024], F32)
            cb = alloc(st, "cbD", [128, 1024], BF16)
            junkD = alloc(st, "junkD", [128, 1024], BF16)
            pt = [alloc(st, f"ptD{i}", [128, 256], F32) for i in range(2)]
            pb = alloc(st, "pbD", [128, 256], BF16)
            slu = [alloc(st, f"sluD{i}", [128, NB], F32) for i in range(2)]
            ssD = alloc(st, "ssD", [128, 1], F32)
            rsD = alloc(st, "rsD", [128, 1], F32)
            tpD = palloc(st, "tpD", [128, 8, 128], BF16)
            gup = [palloc(st, f"gupD{i}", [128, 2, NB], F32) for i in range(3)]
            big = [palloc(st, f"bigD{i}", [128, 1024], F32) for i in range(2)]

            S.op("sp", lambda e: e.dma_start(out=gpf[:], in_=gpf_in[:, :]), writes=["gpf"])
            S.op("sp", lambda e: e.dma_start(out=bpe[:], in_=bpe_in[:, :]), writes=["bpe"])
            nh = 0
            ng = 0
            nbig = 0
            ncp = 0
            for blk in range(NTOK // NB):
                hslots = []
                for t in range(NB // 128):
                    tok = blk * NB + t * 128
                    hs = nh % 2
                    nh += 1
                    hslots.append(hs)
                    S.op("sp", lambda e, hs=hs, tok=tok: e.dma_start(out=h1t[hs][:], in_=h1_d[tok:tok + 128, :]),
                         reads=["h1_d"], writes=[("h1t", hs)])
                    S.op("sp", lambda e, hs=hs, tok=tok: e.dma_start(out=pt[hs][:], in_=pc[tok:tok + 128, :]),
                         writes=[("pt", hs)])
                    S.op("act", lambda e, hs=hs: e.activation(out=junkD[:], in_=h1t[hs][:], func=AF.Square, accum_out=ssD[:]),
                         reads=[("h1t", hs)], writes=["junkD", "ssD"])
                    S.op("act", lambda e: e.activation(out=rsD[:], in_=ssD[:], func=AF.Ln, scale=1.0 / 1024, bias=EPS),
                         reads=["ssD"], writes=["rsD"])
                    S.op("act", lambda e: e.activation(out=rsD[:], in_=rsD[:], func=AF.Exp, scale=-0.5), reads=["rsD"], writes=["rsD"])
                    S.op("dve", lambda e, hs=hs: e.tensor_scalar(out=cb[:], in0=h1t[hs][:], scalar1=rsD[:, 0:1], scalar2=None,
                                                                 op0=ALU.mult),
                         reads=[("h1t", hs), "rsD"], writes=["cb"])
                    for k in range(8):
                        S.op("pe", lambda e, k=k: e.transpose(out=tpD[:, k, :], in_=cb[:, k * 128:(k + 1) * 128], identity=ident[:]),
                             reads=["cb", "ident"], writes=["tpD"])
                    S.op("act", lambda e, t=t: e.copy(out=fT[:, :, t * 128:(t + 1) * 128], in_=tpD[:]),
                         reads=["tpD"], writes=[("fT", t)])
                    S.op("dve", lambda e, hs=hs: e.tensor_copy(out=pb[:], in_=pt[hs][:]), reads=[("pt", hs)], writes=["pb"])
                    for k in range(2):
                        S.op("pe", lambda e, k=k: e.transpose(out=tpD[:, k, :], in_=pb[:, k * 128:(k + 1) * 128], identity=ident[:]),
                             reads=["pb", "ident"], writes=["tpD"])
                    S.op("dve", lambda e, t=t: e.tensor_copy(out=pT[:, :, t * 128:(t + 1) * 128], in_=tpD[:, 0:2, :]),
                         reads=["tpD"], writes=[("pT", t)])
                FT = [("fT", t) for t in range(NB // 128)]
                WGU = [("wD", i) for i in range(32)]
                WDN = [("wD", 32 + i) for i in range(NFF)]
                WPG = [("wD", 32 + NFF + i) for i in range(8)]
                WPP = [("wD", 40 + NFF + i) for i in range(2)]
                for c in range(NFF):
                    gs_ = ng % 3
                    s2 = ng % 2
                    ng += 1
                    for half in range(2):
                        col0 = half * D_FF + c * 128
                        for k in range(8):
                            S.op("pe", lambda e, gs_=gs_, half=half, col0=col0, k=k: e.matmul(
                                gup[gs_][:, half, :], lhsT=wgu[:, k, col0:col0 + 128], rhs=fT[:, k, :],
                                start=(k == 0), stop=(k == 7)),
                                reads=FT + WGU, writes=[("gup", gs_)])
                    S.op("act", lambda e, gs_=gs_, s2=s2: e.activation(out=slu[s2][:], in_=gup[gs_][:, 0, :], func=AF.Silu),
                         reads=[("gup", gs_)], writes=[("slu", s2)])
                    S.op("dve", lambda e, gs_=gs_, s2=s2, c=c: e.tensor_tensor(out=actT[:, c, :], in0=gup[gs_][:, 1, :],
                                                                               in1=slu[s2][:], op=ALU.mult),
                         reads=[("gup", gs_), ("slu", s2)], writes=[("actT", c)])
                ACTT = [("actT", c) for c in range(NFF)]
                ybs = []
                for t in range(NB // 128):
                    yb = nbig % 2
                    nbig += 1
                    ybs.append(yb)
                    for half in range(2):
                        for c in range(NFF):
                            S.op("pe", lambda e, yb=yb, half=half, c=c, t=t: e.matmul(
                                big[yb][:, half * 512:(half + 1) * 512], lhsT=actT[:, c, t * 128:(t + 1) * 128],
                                rhs=wdn[:, c, half * 512:(half + 1) * 512], start=(c == 0), stop=(c == NFF - 1)),
                                reads=ACTT + WDN, writes=[("big", yb)])
                for t in range(NB // 128):
                    tok = blk * NB + t * 128
                    hs = hslots[t]
                    yb = ybs[t]
                    S.op("act", lambda e, yb=yb: e.activation(out=junkD[:], in_=big[yb][:], func=AF.Square, accum_out=ssD[:]),
                         reads=[("big", yb)], writes=["junkD", "ssD"])
                    S.op("act", lambda e: e.activation(out=rsD[:], in_=ssD[:], func=AF.Ln, scale=1.0 / 1024, bias=EPS),
                         reads=["ssD"], writes=["rsD"])
                    S.op("act", lambda e: e.activation(out=rsD[:], in_=rsD[:], func=AF.Exp, scale=-0.5), reads=["rsD"], writes=["rsD"])
                    S.op("dve", lambda e, yb=yb: e.scalar_tensor_tensor(out=bufA[:], in0=big[yb][:], scalar=rsD[:, 0:1], in1=gpf[:],
                                                                        op0=ALU.mult, op1=ALU.mult),
                         reads=[("big", yb), "rsD", "gpf"], writes=["bufA"])
                    S.op("pool", lambda e, hs=hs: e.tensor_tensor(out=h2t[:], in0=bufA[:], in1=h1t[hs][:], op=ALU.add),
                         reads=["bufA", ("h1t", hs)], writes=["h2t"])
                    S.op("dve", lambda e: e.tensor_copy(out=cb[:], in_=h2t[:]), reads=["h2t"], writes=["cb"])
                    for k in range(8):
                        S.op("pe", lambda e, k=k: e.transpose(out=tpD[:, k, :], in_=cb[:, k * 128:(k + 1) * 128], identity=ident[:]),
                             reads=["cb", "ident"], writes=["tpD"])
                    S.op("act", lambda e: e.copy(out=h2T[:], in_=tpD[:]), reads=["tpD"], writes=["h2T"])
                    gb = yb
                    for half in range(2):
                        for k in range(8):
                            S.op("pe", lambda e, gb=gb, half=half, k=k: e.matmul(
                                big[gb][:, half * 512:(half + 1) * 512], lhsT=h2T[:, k, :],
                                rhs=wpg[:, k, half * 512:(half + 1) * 512], start=(k == 0), stop=(k == 7)),
                                reads=["h2T"] + WPG, writes=[("big", gb)])
                    S.op("dve", lambda e, gb=gb: e.tensor_tensor(out=bufB[:], in0=big[gb][:], in1=bpe[:], op=ALU.add),
                         reads=[("big", gb), "bpe"], writes=["bufB"])
                    S.op("act", lambda e: e.activation(out=bufB[:], in_=bufB[:], func=AF.Sigmoid), reads=["bufB"], writes=["bufB"])
                    pbk = yb
                    for half in range(2):
                        for k in range(2):
                            S.op("pe", lambda e, pbk=pbk, half=half, k=k, t=t: e.matmul(
                                big[pbk][:, half * 512:(half + 1) * 512], lhsT=pT[:, k, t * 128:(t + 1) * 128],
                                rhs=wpp[:, k, half * 512:(half + 1) * 512], start=(k == 0), stop=(k == 1)),
                                reads=[("pT", t)] + WPP, writes=[("big", pbk)])
                    S.op("dve", lambda e, pbk=pbk: e.tensor_tensor(out=bufA[:], in0=big[pbk][:], in1=bufB[:], op=ALU.mult),
                         reads=[("big", pbk), "bufB"], writes=["bufA"])
                    S.op("pool", lambda e: e.tensor_tensor(out=bufB[:], in0=bufA[:], in1=h2t[:], op=ALU.add),
                         reads=["bufA", "h2t"], writes=["bufB"])
                    S.op("gq", lambda e, tok=tok: e.dma_start(out=out[tok:tok + 128, :], in_=bufB[:]),
                         reads=["bufB"], writes=["out"])
            S.op("spw", None, reads=["out"])
            S.flush()

    return nc


def _prep_inputs(inputs):
    f = np.float32
    x = np.asarray(inputs["x"], dtype=f)
    p = np.asarray(inputs["p"], dtype=f)
    shared = {
        "w_in": np.ascontiguousarray(inputs["w_in"][0], dtype=f),
        "w_out": np.ascontiguousarray(inputs["w_out"][0], dtype=f),
        "w_gu": np.ascontiguousarray(inputs["w_gate_up"][0], dtype=f),
        "w_dn": np.ascontiguousarray(inputs["w_down"][0], dtype=f),
        "w_pg": np.ascontiguousarray(inputs["w_pe_gate"][0], dtype=f),
        "w_pp": np.ascontiguousarray(inputs["w_pe_proj"][0], dtype=f),
        "wspT": np.ascontiguousarray(np.transpose(np.asarray(inputs["w_spatial"][0], dtype=f), (2, 0, 1))),
        "bsT": np.ascontiguousarray(np.asarray(inputs["b_spatial"][0], dtype=f).T),
        "g_pre": np.ascontiguousarray(np.asarray(inputs["ln_pre_mix"][0], dtype=f).reshape(8, 128).T),
        "g_out": np.ascontiguousarray(np.concatenate([np.asarray(inputs["attn_out_norm"][0], dtype=f),
                                                      np.asarray(inputs["sgu_out_norm"][0], dtype=f)]).reshape(8, 128).T),
        "g_ffn": np.ascontiguousarray(np.asarray(inputs["ln_pre_ffn"][0], dtype=f).reshape(8, 128).T),
        "lng_bc": np.ascontiguousarray(np.broadcast_to(np.tile(np.asarray(inputs["sgu_ln_g"][0], dtype=f), 4)[None, :], (128, 512))),
        "lnb_bc": np.ascontiguousarray(np.broadcast_to(np.tile(np.asarray(inputs["sgu_ln_b"][0], dtype=f), 4)[None, :], (128, 512))),
        "gpm_bc": np.ascontiguousarray(np.broadcast_to(np.asarray(inputs["ln_post_mix"][0], dtype=f)[None, :], (128, 1024))),
        "gpf_bc": np.ascontiguousarray(np.broadcast_to(np.asarray(inputs["ln_post_ffn"][0], dtype=f)[None, :], (128, 1024))),
        "bpe_bc": np.ascontiguousarray(np.broadcast_to(np.asarray(inputs["b_pe_gate"][0], dtype=f)[None, :], (128, 1024))),
    }
    in_maps = []
    for c in range(8):
        b, half = c // 2, c % 2
        t0 = half * NTOK
        if half == 0:
            halo = np.zeros((NHALO, D_MODEL), dtype=f)
        else:
            halo = x[b, t0 - NHALO:t0]
        m = dict(shared)
        m["xc"] = np.ascontiguousarray(np.concatenate([halo, x[b, t0:t0 + NTOK]], axis=0))
        m["pc"] = np.ascontiguousarray(p[0, b, t0:t0 + NTOK])
        m["flag"] = np.full((128, 64), float(half), dtype=f)
        in_maps.append(m)
    return in_maps


_NC_CACHE = {}


def kernel(**inputs):
    in_maps = _prep_inputs(inputs)
    if "nc" not in _NC_CACHE:
        _NC_CACHE["nc"] = build_nc()
    nc = _NC_CACHE["nc"]
    res = run_bass_kernel_spmd(nc, in_maps, core_ids=list(range(8)))
    outp = np.empty((4, 2 * NTOK, D_MODEL), dtype=np.float32)
    for c in range(8):
        b, half = c // 2, c % 2
        outp[b, half * NTOK:(half + 1) * NTOK] = res.results[c]["out"]
    return outp
```
